# Optimizing a Trainium2 kernel written in Bass

```python
import math
import jax, jax.numpy as jnp
from jax import lax
import numpy as np

D_MODEL = 2048
BATCH = 4
SEQ = 2048
DEPTH = 4

NUM_MIXERS = 2
N_SSD_LAYERS = (DEPTH + 1) // 2
N_POOL_LAYERS = DEPTH // 2
N_META = 16
EPS = 1e-6

SSD_EXPAND = 2
D_INNER = SSD_EXPAND * D_MODEL
SSD_HEAD_DIM = 64
SSD_HEADS = D_INNER // SSD_HEAD_DIM
D_STATE = 128
SSD_GROUPS = 8
HEADS_PER_GROUP = SSD_HEADS // SSD_GROUPS
D_CONV = 4
CHUNK = 256
CONV_DIM = D_INNER + 2 * SSD_GROUPS * D_STATE
D_IN_PROJ = D_INNER + CONV_DIM + SSD_HEADS
DT_MIN = 0.001
DT_MAX = 0.1
A_INIT_MAX = 16.0

POOL_WINDOWS = (2, 4, 8, 16)
POOL_GROUPS = len(POOL_WINDOWS)
POOL_GROUP_DIM = D_MODEL // POOL_GROUPS

FFN_HIDDEN = -(-8 * D_MODEL // (3 * 256)) * 256

kernel_name = 'hybrid_ssd_pool_trunk'


def rmsnorm(x, w):
    xf = x.astype(jnp.float32)
    y = xf * lax.rsqrt(jnp.mean(xf * xf, axis=-1, keepdims=True) + EPS)
    return (y * w).astype(x.dtype)


def causal_depthwise_conv(u, w, b):
    c = u.shape[-1]
    out = lax.conv_general_dilated(
        u, w[:, None, :].astype(u.dtype), window_strides=(1,),
        padding=[(w.shape[0] - 1, 0)], dimension_numbers=('NWC', 'WIO', 'NWC'),
        feature_group_count=c)
    return out + b.astype(u.dtype)


def ssd_chunked(X, dt, A, Bm, Cm):
    bsz, l, _, p = X.shape
    nc = l // CHUNK
    G, R, N = SSD_GROUPS, HEADS_PER_GROUP, D_STATE
    Xc = (X * dt[..., None]).reshape(bsz, nc, CHUNK, G, R, p)
    dA = jnp.moveaxis((dt * A).reshape(bsz, nc, CHUNK, G, R), 2, -1)
    dA_cs = jnp.cumsum(dA, axis=-1)
    Bc = Bm.reshape(bsz, nc, CHUNK, G, N)
    Cc = Cm.reshape(bsz, nc, CHUNK, G, N)
    causal = jnp.tril(jnp.ones((CHUNK, CHUNK), dtype=bool))
    seg = dA_cs[..., :, None] - dA_cs[..., None, :]
    Lmat = jnp.exp(jnp.where(causal, seg, -jnp.inf))
    CB = jnp.einsum('bclgn,bcsgn->bcgls', Cc, Bc)
    y_diag = jnp.einsum('bcgls,bcgrls,bcsgrp->bclgrp', CB, Lmat, Xc)
    decay_states = jnp.exp(dA_cs[..., -1:] - dA_cs)
    states = jnp.einsum('bclgn,bcgrl,bclgrp->bcgrpn', Bc, decay_states, Xc)
    chunk_decay = jnp.exp(dA_cs[..., -1])

    def step(carry, inp):
        st, dec = inp
        return carry * dec[..., None, None] + st, carry

    init = jnp.zeros_like(states[:, 0])
    _, prev_states = lax.scan(step, init, (jnp.moveaxis(states, 1, 0), jnp.moveaxis(chunk_decay, 1, 0)))
    prev_states = jnp.moveaxis(prev_states, 0, 1)
    y_off = jnp.einsum('bclgn,bcgrpn,bcgrl->bclgrp', Cc, prev_states, jnp.exp(dA_cs))
    return (y_diag + y_off).reshape(bsz, l, SSD_HEADS, p)


def ssd_mixer(u, w_in, conv_w, conv_b, dt_bias, a_log, d_skip, norm_w, w_out):
    bsz, L, _ = u.shape
    zxbcdt = u @ w_in
    z, xBC, dt_raw = jnp.split(zxbcdt, [D_INNER, D_INNER + CONV_DIM], axis=-1)
    xBC = jax.nn.silu(causal_depthwise_conv(xBC, conv_w, conv_b))
    xs, Bs, Cs = jnp.split(xBC, [D_INNER, D_INNER + SSD_GROUPS * D_STATE], axis=-1)
    dt = jax.nn.softplus(dt_raw.astype(jnp.float32) + dt_bias.astype(jnp.float32))
    A = -jnp.exp(a_log.astype(jnp.float32))
    Xf = xs.astype(jnp.float32).reshape(bsz, L, SSD_HEADS, SSD_HEAD_DIM)
    Bf = Bs.astype(jnp.float32).reshape(bsz, L, SSD_GROUPS, D_STATE)
    Cf = Cs.astype(jnp.float32).reshape(bsz, L, SSD_GROUPS, D_STATE)
    pad_front = (-N_META) % CHUNK
    pad_back = (-(pad_front + L)) % CHUNK

    def pad(t):
        return jnp.pad(t, ((0, 0), (pad_front, pad_back)) + ((0, 0),) * (t.ndim - 2))

    y = ssd_chunked(pad(Xf), pad(dt), A, pad(Bf), pad(Cf))[:, pad_front:pad_front + L]
    y = y + Xf * d_skip.astype(jnp.float32)[:, None]
    y = y.reshape(bsz, L, D_INNER) * jax.nn.silu(z.astype(jnp.float32))
    yg = y.reshape(bsz, L, SSD_GROUPS, D_INNER // SSD_GROUPS)
    yg = yg * lax.rsqrt(jnp.mean(yg * yg, axis=-1, keepdims=True) + EPS)
    y = (yg.reshape(bsz, L, D_INNER) * norm_w).astype(u.dtype)
    return y @ w_out


def pool_mixer(u, w_group, b, scale):
    bsz, L, _ = u.shape
    uf = u.astype(jnp.float32)
    cs = jnp.concatenate([jnp.zeros((bsz, 1, D_MODEL), jnp.float32), jnp.cumsum(uf, axis=1)], axis=1)
    t = jnp.arange(L)
    pooled = []
    for g, win in enumerate(POOL_WINDOWS):
        cs_g = cs[..., g * POOL_GROUP_DIM:(g + 1) * POOL_GROUP_DIM]
        start = jnp.maximum(t + 1 - win, 0)
        count = (t + 1 - start).astype(jnp.float32)
        pooled.append((cs_g[:, 1:] - cs_g[:, start]) / count[None, :, None])
    pooled = jnp.stack(pooled, axis=2)
    mixed = (pooled - uf.reshape(bsz, L, POOL_GROUPS, POOL_GROUP_DIM)).astype(u.dtype)
    out = jnp.einsum('blgc,gcd->blgd', mixed, w_group) + b.reshape(POOL_GROUPS, POOL_GROUP_DIM)
    return out.reshape(bsz, L, D_MODEL) * scale


def swiglu(u, w_gate, w_up, w_down):
    return (jax.nn.silu(u @ w_gate) * (u @ w_up)) @ w_down


def setup_inputs(seed: int = 0) -> dict:
    key = jax.random.key(seed)
    ks = jax.random.split(key, 17)
    f32 = jnp.float32

    def nrm(k, shape, scale):
        return scale * jax.random.normal(k, shape, f32)

    x = jax.random.normal(ks[0], (BATCH, SEQ, D_MODEL), f32)
    meta_tokens = nrm(ks[1], (N_META, D_MODEL), 1.0)
    norm_w = 1.0 + nrm(ks[2], (DEPTH, 4, D_MODEL), 0.05)
    ssd_w_in = nrm(ks[3], (N_SSD_LAYERS, D_MODEL, D_IN_PROJ), D_MODEL ** -0.5)
    ssd_conv_w = nrm(ks[4], (N_SSD_LAYERS, D_CONV, CONV_DIM), D_CONV ** -0.5)
    ssd_conv_b = nrm(ks[5], (N_SSD_LAYERS, CONV_DIM), 0.01)
    dt0 = jnp.exp(jax.random.uniform(ks[6], (N_SSD_LAYERS, SSD_HEADS), f32,
                                     math.log(DT_MIN), math.log(DT_MAX)))
    ssd_dt_bias = dt0 + jnp.log(-jnp.expm1(-dt0))
    ssd_a_log = jnp.log(jax.random.uniform(ks[7], (N_SSD_LAYERS, SSD_HEADS), f32, 1.0, A_INIT_MAX))
    ssd_d = 1.0 + nrm(ks[8], (N_SSD_LAYERS, SSD_HEADS), 0.1)
    ssd_norm_w = 1.0 + nrm(ks[9], (N_SSD_LAYERS, D_INNER), 0.05)
    ssd_w_out = nrm(ks[10], (N_SSD_LAYERS, D_INNER, D_MODEL), D_INNER ** -0.5)
    pool_w = nrm(ks[11], (N_POOL_LAYERS, POOL_GROUPS, POOL_GROUP_DIM, POOL_GROUP_DIM), POOL_GROUP_DIM ** -0.5)
    pool_b = nrm(ks[12], (N_POOL_LAYERS, D_MODEL), 0.01)
    pool_scale = 1.0 + nrm(ks[13], (N_POOL_LAYERS, D_MODEL), 0.1)
    ffn_w_gate = nrm(ks[14], (DEPTH, D_MODEL, FFN_HIDDEN), D_MODEL ** -0.5)
    ffn_w_up = nrm(ks[15], (DEPTH, D_MODEL, FFN_HIDDEN), D_MODEL ** -0.5)
    ffn_w_down = nrm(ks[16], (DEPTH, FFN_HIDDEN, D_MODEL), FFN_HIDDEN ** -0.5)
    return {'x': x, 'meta_tokens': meta_tokens, 'norm_w': norm_w,
            'ssd_w_in': ssd_w_in, 'ssd_conv_w': ssd_conv_w, 'ssd_conv_b': ssd_conv_b,
            'ssd_dt_bias': ssd_dt_bias, 'ssd_a_log': ssd_a_log, 'ssd_d': ssd_d,
            'ssd_norm_w': ssd_norm_w, 'ssd_w_out': ssd_w_out,
            'pool_w': pool_w, 'pool_b': pool_b, 'pool_scale': pool_scale,
            'ffn_w_gate': ffn_w_gate, 'ffn_w_up': ffn_w_up, 'ffn_w_down': ffn_w_down}


def reference(x, meta_tokens, norm_w, ssd_w_in, ssd_conv_w, ssd_conv_b, ssd_dt_bias,
              ssd_a_log, ssd_d, ssd_norm_w, ssd_w_out, pool_w, pool_b, pool_scale,
              ffn_w_gate, ffn_w_up, ffn_w_down):
    bsz = x.shape[0]
    meta = jnp.broadcast_to(meta_tokens[None].astype(x.dtype), (bsz, N_META, D_MODEL))
    h = jnp.concatenate([meta, x], axis=1)
    for i in range(DEPTH):
        j = i // NUM_MIXERS
        u = rmsnorm(h, norm_w[i, 0])
        if i % NUM_MIXERS == 0:
            mix = ssd_mixer(u, ssd_w_in[j], ssd_conv_w[j], ssd_conv_b[j], ssd_dt_bias[j],
                            ssd_a_log[j], ssd_d[j], ssd_norm_w[j], ssd_w_out[j])
        else:
            mix = pool_mixer(u, pool_w[j], pool_b[j], pool_scale[j])
        h = h + rmsnorm(mix, norm_w[i, 1])
        f = swiglu(rmsnorm(h, norm_w[i, 2]), ffn_w_gate[i], ffn_w_up[i], ffn_w_down[i])
        h = h + rmsnorm(f, norm_w[i, 3])
    return h[:, N_META:]
```

```python
import numpy as np
from contextlib import ExitStack
import concourse.bass as bass
import concourse.mybir as mybir
from concourse.bass_utils import run_bass_kernel_spmd

F32 = mybir.dt.float32
BF16 = mybir.dt.bfloat16
AF = mybir.ActivationFunctionType
ALU = mybir.AluOpType

D = 2048
NMETA = 16
DI = 4096
NH = 64
HD = 64
NS = 128
NG = 8
FH = 5632
DINP = 10304
EPS = 1e-6
DEPTH = 4
ENGINES = ["sync", "scalar", "gpsimd", "vector", "tensor"]
BAR_ENGS = ["sync", "scalar", "vector", "tensor"]


class Prog:
    def __init__(self, nc, n_dma_sems=8):
        self.nc = nc
        self.ops = []
        self.n_dma_sems = n_dma_sems

    def op(self, eng, fn, reads=(), writes=(), dma=False, banks=(), barrier=False):
        self.ops.append(dict(eng=eng, fn=fn, reads=tuple(reads), writes=tuple(writes),
                             dma=dma, banks=tuple(banks), barrier=barrier))

    def dma(self, eng, out, in_, reads=(), writes=()):
        self.op(eng, lambda e: e.dma_start(out=out, in_=in_), reads, writes, dma=True)

    def barrier(self):
        for e in BAR_ENGS:
            self.op(e, None, barrier=True)

    def emit(self, stack):
        nc = self.nc
        ops = self.ops
        n = len(ops)
        dma_count = {e: 0 for e in ENGINES}
        stream = [None] * n
        for i, o in enumerate(ops):
            if o["dma"]:
                k = dma_count[o["eng"]]
                dma_count[o["eng"]] += 1
                stream[i] = ("dma", o["eng"], k % self.n_dma_sems)
            else:
                stream[i] = o["eng"]
        last_w = {}
        readers = {}
        deps = [None] * n
        bank_last = {}
        last_on_stream = {}
        for i, o in enumerate(ops):
            d = set()
            if o["barrier"]:
                for s, j in last_on_stream.items():
                    if s == "gpsimd" or (isinstance(s, tuple) and s[1] == "gpsimd"):
                        continue
                    d.add(j)
            for b in o["banks"]:
                bl = bank_last.setdefault(b, {})
                for e2, j in bl.items():
                    if e2 != o["eng"]:
                        d.add(j)
                bl[o["eng"]] = i
            for t in o["reads"]:
                if t in last_w:
                    d.add(last_w[t])
            for t in o["writes"]:
                if t in last_w:
                    d.add(last_w[t])
                for r in readers.get(t, ()):
                    d.add(r)
            if o["dma"]:
                p = last_on_stream.get(stream[i])
                if p is not None:
                    d.add(p)
            d.discard(i)
            deps[i] = d
            for t in o["reads"]:
                readers.setdefault(t, []).append(i)
            for t in o["writes"]:
                last_w[t] = i
                readers[t] = []
            if not o["barrier"]:
                last_on_stream[stream[i]] = i
        needed = [bool(o["dma"]) for o in ops]
        red = [None] * n
        seen = {e: {} for e in ENGINES}
        for i, o in enumerate(ops):
            e = o["eng"]
            best = {}
            for d in deps[i]:
                s = stream[d]
                if s == "tensor" and e == "tensor" and not o["dma"]:
                    continue
                if d > best.get(s, -1):
                    best[s] = d
            lst = []
            for s, d in best.items():
                if seen[e].get(s, -1) >= d:
                    continue
                seen[e][s] = d
                lst.append(d)
                needed[d] = True
            red[i] = lst
        sems = {}
        for s in sorted(set(stream), key=str):
            nm = "s_" + ("_".join(str(x) for x in s) if isinstance(s, tuple) else s)
            sems[s] = stack.enter_context(nc.semaphore(nm))
        cnt = {s: 0 for s in sems}
        val = [0] * n
        for i, o in enumerate(ops):
            if needed[i]:
                s = stream[i]
                cnt[s] += 16 if o["dma"] else 1
                val[i] = cnt[s]
        self.max_sem = max(cnt.values()) if cnt else 0
        block = stack.enter_context(nc.Block())
        per_eng = {e: [] for e in ENGINES}
        for i, o in enumerate(ops):
            per_eng[o["eng"]].append(i)

        def make(e_name):
            def body(eng):
                for i in per_eng[e_name]:
                    o = ops[i]
                    for d in red[i]:
                        eng.wait_ge(sems[stream[d]], val[d])
                    if o["fn"] is None:
                        assert not needed[i]
                        continue
                    ins = o["fn"](eng)
                    if needed[i]:
                        ins.then_inc(sems[stream[i]], 16 if o["dma"] else 1)
            return body

        for e_name in ENGINES:
            if per_eng[e_name]:
                getattr(block, e_name)(make(e_name))


class Rot:
    def __init__(self, items):
        self.items = items
        self.i = 0

    def next(self):
        it = self.items[self.i % len(self.items)]
        self.i += 1
        return it


def make_parts(nx):
    tiles = [(0, NMETA)] + [(NMETA + 128 * i, 128) for i in range(nx // 128)]
    parts = [tiles[0:5]]
    k = 5
    while k < len(tiles):
        parts.append(tiles[k:k + 4])
        k += 4
    return parts


def build(nx=2048, layers=DEPTH, dbg=None):
    nc = bass.Bass("TRN2", target_bir_lowering=False)
    din = lambda name, shape: nc.dram_tensor(name, list(shape), F32, kind="ExternalInput").ap()
    x_d = din("x", [nx, D])
    meta_d = din("meta", [NMETA, D])
    normw_d = din("norm_w", [DEPTH, 4, D])
    win_d = din("ssd_w_in", [2, D, DINP])
    wout_d = din("ssd_w_out", [2, DI, D])
    poolw_d = din("pool_w", [2, 4, 512, 512])
    wg_d = din("ffn_w_gate", [DEPTH, D, FH])
    wu_d = din("ffn_w_up", [DEPTH, D, FH])
    wd_d = din("ffn_w_down", [DEPTH, FH, D])
    cw_d = din("ssd_cw", [2, 128, 48 * 4])
    cb_d = din("ssd_cb", [2, 128, 48])
    dtb_d = din("ssd_dt_bias", [2, NH])
    alog_d = din("ssd_a_log", [2, NH])
    dsk_d = din("ssd_d", [2, NH])
    snw_d = din("ssd_norm_w", [2, DI])
    pb_d = din("pool_b", [2, D])
    psc_d = din("pool_scale", [2, D])
    cst_d = din("cst", [128, 512])
    pcnt_d = din("pcnt", [1, 64])
    y_d = nc.dram_tensor("y", [nx, D], F32, kind="ExternalOutput").ap()
    st_d = nc.dram_tensor("ssd_state", [2 * NG * 128, 512], F32, kind="Internal").ap()
    stage = (dbg or {}).get("stage", 99)

    parts = make_parts(nx)
    SMAX = 528
    TMAX = 5

    with ExitStack() as st:
        sb = lambda name, shape, dt: st.enter_context(nc.sbuf_tensor(name, shape, dt))
        P = Prog(nc)
        H = sb("H", [128, TMAX, D], F32)
        FB = sb("FB", [128, TMAX * D], F32)
        UT = sb("UT", [128, 16, SMAX], BF16)
        BIG = sb("BIG", [128, 44 * SMAX], BF16)
        WBs = [sb(f"WB{i}", [128, 4096], BF16) for i in range(4)]
        WBC = [sb(f"WBC{i}", [128, D], F32) for i in range(2)]
        CST = sb("CST", [128, 512], F32)
        identb = sb("identb", [128, 128], BF16)
        maskb = sb("maskb", [128, 128], BF16)
        ssq = sb("ssq", [128, 8], F32)
        rsd = sb("rsd", [128, 8], F32)
        sgt = [sb(f"sgt{i}", [128, 512], BF16) for i in range(2)]
        convhalo = sb("convhalo", [128, 2 * 48 * 3], F32)
        poolhalo = sb("poolhalo", [128, 2 * 16 * 15], F32)
        ssdc = sb("ssdc", [128, 2 * 3 * NH], F32)
        cwt = sb("cwt", [128, 2 * 48 * 4], F32)
        cbt = sb("cbt", [128, 2 * 48], F32)
        pcn = sb("pcn", [128, 64], F32)
        banks = [st.enter_context(nc.psum_tensor(f"pb{i}", [128, 512], F32)) for i in range(8)]
        ident32 = CST[:, 0:128]
        tri32 = CST[:, 128:256]
        ones32 = CST[:, 256:384]
        mask32 = CST[:, 384:512]

        bank_i = [0]

        def nb():
            b = bank_i[0] % 8
            bank_i[0] += 1
            return b

        wb_rot = Rot([(WBs[i], ("WB", i)) for i in range(4)])
        wbc_rot = Rot([(WBC[i], ("WBC", i)) for i in range(2)])
        FBb = FB[:].bitcast(BF16)
        utok_rot = Rot([(FBb[:, (3 + i) * 2 * D:(3 + i) * 2 * D + D], ("FB", 3 + i)) for i in range(2)])
        sgt_rot = Rot([(sgt[i], ("sgt", i)) for i in range(2)])
        evac_i = [0]

        def evac_eng():
            evac_i[0] += 1
            return "vector" if evac_i[0] % 2 else "scalar"

        def copy_op(eng, out, in_, reads, writes, banks=()):
            if eng == "vector":
                P.op("vector", lambda e: e.tensor_copy(out=out, in_=in_), reads, writes, banks=banks)
            else:
                P.op("scalar", lambda e: e.copy(out=out, in_=in_), reads, writes, banks=banks)

        P.dma("sync", CST[:], cst_d, writes=["CST"])
        P.op("vector", lambda e: e.tensor_copy(out=identb[:], in_=ident32), reads=["CST"], writes=["identb"])
        P.op("vector", lambda e: e.tensor_copy(out=maskb[:], in_=mask32), reads=["CST"], writes=["maskb"])
        P.dma("sync", cwt[:].rearrange("p (l c) -> p l c", l=2), cw_d.rearrange("l p c -> p l c"), writes=["cwt"])
        P.dma("sync", cbt[:].rearrange("p (l c) -> p l c", l=2), cb_d.rearrange("l p c -> p l c"), writes=["cbt"])
        for l in range(2):
            P.dma("sync", ssdc[:, (l * 3 + 0) * NH:(l * 3 + 1) * NH], alog_d[l:l + 1, :].partition_broadcast(128), writes=[("ssdc", l)])
            P.dma("sync", ssdc[:, (l * 3 + 1) * NH:(l * 3 + 2) * NH], dsk_d[l:l + 1, :].partition_broadcast(128), writes=[("ssdc", l)])
            P.dma("sync", ssdc[:, (l * 3 + 2) * NH:(l * 3 + 3) * NH], dtb_d[l:l + 1, :].partition_broadcast(128), writes=[("ssdc", l)])
            an = ssdc[:, (l * 3) * NH:(l * 3 + 1) * NH]
            P.op("scalar", lambda e, an=an: e.activation(out=an, in_=an, func=AF.Exp), reads=[("ssdc", l)], writes=[("ssdc", l)])
            P.op("vector", lambda e, an=an: e.tensor_scalar(out=an, in0=an, scalar1=-1.0, scalar2=None, op0=ALU.mult), reads=[("ssdc", l)], writes=[("ssdc", l)])
        P.dma("sync", pcn[:], pcnt_d.partition_broadcast(128), writes=["pcn"])
        P.op("vector", lambda e: e.memset(convhalo[:], 0.0), writes=[("chalo", l, c) for l in range(2) for c in range(48)])
        P.op("vector", lambda e: e.memset(poolhalo[:], 0.0), writes=["poolhalo"])
        P.op("vector", lambda e: e.memset(ssq[:], 0.0), writes=["ssq"])

        def sumsq(src_ap, r, col, reads, junk, junk_toks):
            P.op("scalar", lambda e: e.activation(out=junk, in_=src_ap, func=AF.Square,
                                                  accum_out=ssq[:r, col:col + 1]),
                 reads=reads + ["ssq"], writes=list(junk_toks) + [("ssq", col)])

        def rstd(r, col, n):
            P.op("vector", lambda e: e.tensor_scalar(out=rsd[:r, col:col + 1], in0=ssq[:r, col:col + 1], scalar1=1.0 / n,
                                                     scalar2=EPS, op0=ALU.mult, op1=ALU.add),
                 reads=[("ssq", col)], writes=[("rsd", col)])
            P.op("scalar", lambda e: e.activation(out=rsd[:r, col:col + 1], in_=rsd[:r, col:col + 1], func=AF.Sqrt),
                 reads=[("rsd", col)], writes=[("rsd", col)])
            P.op("vector", lambda e: e.reciprocal(out=rsd[:r, col:col + 1], in_=rsd[:r, col:col + 1]),
                 reads=[("rsd", col)], writes=[("rsd", col)])

        def load_wbc(row_ap):
            w, tok = wbc_rot.next()
            P.dma("sync", w[:], row_ap.partition_broadcast(128), writes=[tok])
            return w, tok

        def prenorm(part, li, j, fp32_dst=None):
            w, wtok = load_wbc(normw_d[li, j:j + 1, :])
            col = 0
            for ti, (row0, r) in enumerate(part):
                sumsq(H[:r, ti, :], r, ti, [("H", ti)], FBb[:r, 2 * 2 * D:2 * 2 * D + D], [("FB", 2)])
                rstd(r, ti, D)
                if fp32_dst is None:
                    ut, uttok = utok_rot.next()
                    P.op("vector", lambda e, ut=ut, ti=ti, r=r: e.scalar_tensor_tensor(
                        out=ut[:r, :], in0=H[:r, ti, :], scalar=rsd[:r, ti:ti + 1], in1=w[:r, :], op0=ALU.mult, op1=ALU.mult),
                        reads=[("H", ti), ("rsd", ti), wtok], writes=[uttok])
                    for half in range(2):
                        b = nb()
                        pbv = banks[b][:].bitcast(BF16)
                        for kk in range(8):
                            k = half * 8 + kk
                            P.op("tensor", lambda e, pbv=pbv, kk=kk, k=k, ut=ut, r=r: e.transpose(
                                out=pbv[:, kk * 128:kk * 128 + r], in_=ut[:r, k * 128:(k + 1) * 128], identity=identb[:r, :r]),
                                reads=[uttok, "identb"], writes=[("pb", b)], banks=[b])
                        copy_op(evac_eng(), UT[:, half * 8:half * 8 + 8, col:col + r],
                                pbv[:, 0:1024].rearrange("p (k t) -> p k t", k=8)[:, :, 0:r],
                                reads=[("pb", b)], writes=[("UT", ti)], banks=[b])
                else:
                    u32 = FB[:, 0:D]
                    P.op("vector", lambda e, ti=ti, r=r: e.scalar_tensor_tensor(
                        out=u32[:r, :], in0=H[:r, ti, :], scalar=rsd[:r, ti:ti + 1], in1=w[:r, :], op0=ALU.mult, op1=ALU.mult),
                        reads=[("H", ti), ("rsd", ti), wtok], writes=["u32"])
                    for q in range(4):
                        b = nb()
                        for kk in range(4):
                            k = q * 4 + kk
                            P.op("tensor", lambda e, b=b, kk=kk, k=k, r=r: e.transpose(
                                out=banks[b][:, kk * 128:kk * 128 + r], in_=u32[:r, k * 128:(k + 1) * 128], identity=ident32[:r, :r]),
                                reads=["u32", "CST"], writes=[("pb", b)], banks=[b])
                        copy_op(evac_eng(), fp32_dst[:, q * 4:q * 4 + 4, 15 + col:15 + col + r],
                                banks[b][:].rearrange("p (k t) -> p k t", k=4)[:, :, 0:r],
                                reads=[("pb", b)], writes=[("UT32", ti)], banks=[b])
                col += r

        def postnorm(part, li, j):
            w, wtok = load_wbc(normw_d[li, j:j + 1, :])
            for ti, (row0, r) in enumerate(part):
                fb = FB[:r, ti * D:(ti + 1) * D]
                sumsq(fb.rearrange("p (a b) -> p a b", a=4), r, ti, [("FB", ti)], UT[:r, 0:4, 0:512],
                      [("UT", q) for q in range(TMAX)] + [("MIX", q) for q in range(16)])
                rstd(r, ti, D)
                P.op("vector", lambda e, fb=fb, ti=ti, r=r: e.scalar_tensor_tensor(
                    out=fb, in0=fb, scalar=rsd[:r, ti:ti + 1], in1=w[:r, :], op0=ALU.mult, op1=ALU.mult),
                    reads=[("FB", ti), ("rsd", ti), wtok], writes=[("FB", ti)])
                P.op("vector", lambda e, fb=fb, ti=ti, r=r: e.tensor_tensor(out=H[:r, ti, :], in0=H[:r, ti, :], in1=fb, op=ALU.add),
                     reads=[("FB", ti), ("H", ti)], writes=[("H", ti)])

        def tokgroups(S):
            if S <= 512:
                return [(0, S)]
            h = (S // 2 + 7) // 8 * 8
            return [(0, h), (h, S)]

        def ut_reads(part, n0, n1):
            res = []
            col = 0
            for ti, (row0, r) in enumerate(part):
                if col < n1 and col + r > n0:
                    res.append(("UT", ti))
                col += r
            return res

        def wload(src_ap, shape3):
            w, tok = wb_rot.next()
            a, b_ = shape3
            view = w[:, 0:a * b_].rearrange("p (a b) -> p a b", a=a)
            P.dma("gpsimd", view, src_ap, writes=[tok])
            return view, tok

        def ffn(part, li):
            S = sum(r for _, r in part)
            prenorm(part, li, 2)
            H1T = BIG[:, 0:44 * SMAX].rearrange("p (c t) -> p c t", c=44)
            wgv = wg_d[li].rearrange("(k p) n -> p k n", p=128)
            wuv = wu_d[li].rearrange("(k p) n -> p k n", p=128)
            for jb in range(FH // 256):
                wg, wgtok = wload(wgv[:, :, jb * 256:(jb + 1) * 256], (16, 256))
                wu, wutok = wload(wuv[:, :, jb * 256:(jb + 1) * 256], (16, 256))
                for c in range(2):
                    ch = jb * 2 + c
                    for (n0, n1) in tokgroups(S):
                        ba, bb = nb(), nb()
                        ur = ut_reads(part, n0, n1)
                        for k in range(16):
                            P.op("tensor", lambda e, ba=ba, k=k, c=c, wg=wg, n0=n0, n1=n1: e.matmul(
                                banks[ba][:, 0:n1 - n0], lhsT=wg[:, k, c * 128:(c + 1) * 128], rhs=UT[:, k, n0:n1],
                                start=(k == 0), stop=(k == 15)), reads=[wgtok] + ur, writes=[("pb", ba)], banks=[ba])
                        for k in range(16):
                            P.op("tensor", lambda e, bb=bb, k=k, c=c, wu=wu, n0=n0, n1=n1: e.matmul(
                                banks[bb][:, 0:n1 - n0], lhsT=wu[:, k, c * 128:(c + 1) * 128], rhs=UT[:, k, n0:n1],
                                start=(k == 0), stop=(k == 15)), reads=[wutok] + ur, writes=[("pb", bb)], banks=[bb])
                        sg, sgtok = sgt_rot.next()
                        P.op("scalar", lambda e, sg=sg, ba=ba, n0=n0, n1=n1: e.activation(
                            out=sg[:, 0:n1 - n0], in_=banks[ba][:, 0:n1 - n0], func=AF.Silu),
                            reads=[("pb", ba)], writes=[sgtok], banks=[ba])
                        P.op("vector", lambda e, sg=sg, bb=bb, ch=ch, n0=n0, n1=n1: e.tensor_tensor(
                            out=H1T[:, ch, n0:n1], in0=sg[:, 0:n1 - n0], in1=banks[bb][:, 0:n1 - n0], op=ALU.mult),
                            reads=[sgtok, ("pb", bb)], writes=[("H1T", ch)], banks=[bb])
            wdv = wd_d[li].rearrange("(k p) n -> p k n", p=128)
            for nblk in range(4):
                accs = [nb() for _ in part]
                for pc in range(6):
                    k0 = pc * 8
                    kn = min(8, 44 - k0)
                    wd, wdtok = wload(wdv[:, k0:k0 + kn, nblk * 512:(nblk + 1) * 512], (kn, 512))
                    for kk in range(kn):
                        ch = k0 + kk
                        col = 0
                        for ti, (row0, r) in enumerate(part):
                            P.op("tensor", lambda e, a=accs[ti], ch=ch, kk=kk, wd=wd, col=col, r=r: e.matmul(
                                banks[a][:r, :], lhsT=H1T[:, ch, col:col + r], rhs=wd[:, kk, :],
                                start=(ch == 0), stop=(ch == 43)), reads=[wdtok, ("H1T", ch)], writes=[("pb", accs[ti])], banks=[accs[ti]])
                            col += r
                for ti, (row0, r) in enumerate(part):
                    copy_op(evac_eng(), FB[:r, ti * D + nblk * 512: ti * D + (nblk + 1) * 512], banks[accs[ti]][:r, :],
                            reads=[("pb", accs[ti])], writes=[("FB", ti)], banks=[accs[ti]])
            postnorm(part, li, 3)

        def pool(part, li, pidx):
            lj = li // 2
            S = sum(r for _, r in part)
            U32 = BIG[:].bitcast(F32)[:, 0:16 * (15 + SMAX)].rearrange("p (c t) -> p c t", c=16)
            halo = poolhalo[:, lj * 240:(lj + 1) * 240].rearrange("p (c t) -> p c t", c=16)
            P.op("vector", lambda e: e.tensor_copy(out=U32[:, :, 0:15], in_=halo), reads=["poolhalo"], writes=["U32h"])
            prenorm(part, li, 0, fp32_dst=U32)
            allut = [("UT32", ti) for ti in range(len(part))]
            P.op("vector", lambda e: e.tensor_copy(out=halo, in_=U32[:, :, S:S + 15]), reads=allut + ["U32h"], writes=["poolhalo"])
            tmpA = FB[:, D:D + 15 + SMAX]
            tmpB = FB[:, D + 1024:D + 1024 + 15 + SMAX]
            for c in range(16):
                g = c // 4
                nlev = g + 1
                src = U32[:, c, :]
                cur = src
                lo = 0
                tmps = [tmpA, tmpB]
                for lev in range(nlev):
                    sh = 1 << lev
                    dst = tmps[lev % 2]
                    lo2 = lo + sh
                    P.op("vector", lambda e, dst=dst, cur=cur, lo2=lo2, sh=sh: e.tensor_tensor(
                        out=dst[:, lo2:15 + S], in0=cur[:, lo2:15 + S], in1=cur[:, lo2 - sh:15 + S - sh], op=ALU.add),
                        reads=allut + ["U32h", "ptmp"], writes=["ptmp"])
                    cur = dst
                    lo = lo2
                if pidx == 0:
                    P.op("vector", lambda e, cur=cur, g=g: e.tensor_tensor(
                        out=cur[:, 15:31], in0=cur[:, 15:31], in1=pcn[:, g * 16:(g + 1) * 16], op=ALU.mult),
                        reads=["ptmp", "pcn"], writes=["ptmp"])
                if True:
                    P.op("vector", lambda e, cur=cur, c=c, src=src, g=g: e.scalar_tensor_tensor(
                        out=UT[:, c, 0:S], in0=cur[:, 15:15 + S], scalar=1.0 / (2 << g), in1=src[:, 15:15 + S],
                        op0=ALU.mult, op1=ALU.subtract),
                        reads=["ptmp"] + allut, writes=[("MIX", c)])
            pws = [wload(poolw_d[lj, 2 * q:2 * q + 2].rearrange("g (k p) n -> p (g k) n", p=128), (8, 512)) for q in range(2)]
            bw, bwtok = load_wbc(pb_d[lj:lj + 1, :])
            sw, swtok = load_wbc(psc_d[lj:lj + 1, :])
            P.barrier()
            col = 0
            for ti, (row0, r) in enumerate(part):
                for g in range(4):
                    b = nb()
                    pw, pwtok = pws[g // 2]
                    for kk in range(4):
                        P.op("tensor", lambda e, b=b, g=g, kk=kk, col=col, r=r, pw=pw: e.matmul(
                            banks[b][:r, :], lhsT=UT[:, g * 4 + kk, col:col + r], rhs=pw[:, (g % 2) * 4 + kk, :],
                            start=(kk == 0), stop=(kk == 3)), reads=[pwtok] + [("MIX", g * 4 + kk)], writes=[("pb", b)], banks=[b])
                    fb = FB[:r, ti * D + g * 512: ti * D + (g + 1) * 512]
                    P.op("vector", lambda e, fb=fb, b=b, g=g, r=r: e.tensor_tensor(
                        out=fb, in0=banks[b][:r, :], in1=bw[:r, g * 512:(g + 1) * 512], op=ALU.add),
                        reads=[("pb", b), bwtok], writes=[("FB", ti)], banks=[b])
                    P.op("vector", lambda e, fb=fb, g=g, r=r: e.tensor_tensor(
                        out=fb, in0=fb, in1=sw[:r, g * 512:(g + 1) * 512], op=ALU.mult),
                        reads=[("FB", ti), swtok], writes=[("FB", ti)])
                col += r
            postnorm(part, li, 1)

        def ssd(part, li, pidx, last_part):
            lj = li // 2
            S = sum(r for _, r in part)
            T = len(part)
            cols = []
            c_ = 0
            for (_, r) in part:
                cols.append(c_)
                c_ += r
            prenorm(part, li, 0)
            P.barrier()
            if stage <= 1:
                return
            allut = [("UT", ti) for ti in range(T)]
            off = [0]

            def carve(nelem, dt):
                nbytes = nelem * (4 if dt == F32 else 2)
                o = off[0]
                off[0] += (nbytes + 31) // 32 * 32
                assert off[0] <= TMAX * D * 4
                if dt == F32:
                    return FB[:, o // 4:o // 4 + nelem]
                return FB[:].bitcast(BF16)[:, o // 2:o // 2 + nelem]

            DT = carve(T * NH, F32)
            DA = carve(T * NH, F32)
            CS = carve(T * NH, F32)
            NCS = carve(T * NH, F32)
            ECS = carve(T * NH, F32)
            DST = carve(T * NH, F32)
            CDEC = carve(T * NH, F32)
            TMP64 = carve(NH, F32)
            TMP64b = carve(NH, F32)
            ZS = carve(T * 512, BF16)
            XPRE = [carve(3 + SMAX, F32) for _ in range(2)]
            CACC = [carve(SMAX, F32) for _ in range(1)]
            XT = carve(4 * SMAX, BF16)
            BT = carve(SMAX, BF16)
            CT = carve(SMAX, BF16)
            XTOK = [carve(512, BF16) for _ in range(2)]
            XDT = [carve(512, BF16) for _ in range(2)]
            XDS = [carve(512, BF16) for _ in range(1)]
            BTOK = [carve(128, BF16) for _ in range(2)]
            CBT = [carve(128, BF16) for _ in range(2)]
            EE = [carve(128, BF16) for _ in range(4)]
            MT = [carve(128, BF16) for _ in range(4)]
            bo = 32 * SMAX * 2
            BIGf = BIG[:].bitcast(F32)
            T1 = [BIGf[:, bo // 4 + i * 512: bo // 4 + (i + 1) * 512] for i in range(2)]
            T2 = [BIGf[:, bo // 4 + 1024: bo // 4 + 1536]]
            YN = [BIG[:, bo // 2 + 3072 + i * 512: bo // 2 + 3072 + (i + 1) * 512] for i in range(2)]
            assert bo + 6144 + 2048 <= 44 * SMAX * 2
            SST = carve(512, F32)
            SBF = carve(512, BF16)
            NWG = [carve(512, F32) for _ in range(1)]
            rot = lambda lst, nm: Rot([(lst[i], (nm, i)) for i in range(len(lst))])
            xpre_rot, cacc_rot = rot(XPRE, "XPRE"), rot(CACC, "CACC")
            xtok_rot, xdt_rot, xds_rot = rot(XTOK, "XTOK"), rot(XDT, "XDT"), rot(XDS, "XDS")
            btok_rot, cbt_rot, ee_rot, mt_rot = rot(BTOK, "BTOK"), rot(CBT, "CBT"), rot(EE, "EE"), rot(MT, "MT")
            t1_rot, t2_rot, yn_rot, nwg_rot = rot(T1, "T1"), rot(T2, "T2"), rot(YN, "YN"), rot(NWG, "NWG")
            YT = BIG[:, 0:32 * SMAX].rearrange("p (c t) -> p c t", c=32)
            aneg = ssdc[:, (lj * 3) * NH:(lj * 3 + 1) * NH]
            dskip = ssdc[:, (lj * 3 + 1) * NH:(lj * 3 + 2) * NH]
            dtbias = ssdc[:, (lj * 3 + 2) * NH:(lj * 3 + 3) * NH]
            winv = win_d[lj].rearrange("(k p) n -> p k n", p=128)
            wdt, wdttok = wload(winv[:, :, DI + 6144:DINP], (16, NH))
            for ti, (row0, r) in enumerate(part):
                c0 = cols[ti]
                sl = slice(ti * NH, (ti + 1) * NH)
                b = nb()
                for k in range(16):
                    P.op("tensor", lambda e, b=b, k=k, c0=c0, r=r: e.matmul(
                        banks[b][:r, 0:NH], lhsT=UT[:, k, c0:c0 + r], rhs=wdt[:, k, :], start=(k == 0), stop=(k == 15)),
                        reads=[wdttok, ("UT", ti)], writes=[("pb", b)], banks=[b])
                P.op("vector", lambda e, b=b, r=r: e.tensor_tensor(out=TMP64[:r, :], in0=banks[b][:r, 0:NH], in1=dtbias[:r, :], op=ALU.add),
                     reads=[("pb", b), ("ssdc", lj)], writes=["TMP64"], banks=[b])
                P.op("scalar", lambda e, r=r: e.activation(out=TMP64b[:r, :], in_=TMP64[:r, :], func=AF.Abs),
                     reads=["TMP64"], writes=["TMP64b"])
                P.op("scalar", lambda e, r=r: e.activation(out=TMP64b[:r, :], in_=TMP64b[:r, :], func=AF.Exp, scale=-1.0),
                     reads=["TMP64b"], writes=["TMP64b"])
                P.op("scalar", lambda e, r=r: e.activation(out=TMP64b[:r, :], in_=TMP64b[:r, :], func=AF.Ln, bias=1.0),
                     reads=["TMP64b"], writes=["TMP64b"])
                P.op("vector", lambda e, r=r, sl=sl: e.scalar_tensor_tensor(
                    out=DT[:r, sl], in0=TMP64[:r, :], scalar=0.0, in1=TMP64b[:r, :], op0=ALU.max, op1=ALU.add),
                    reads=["TMP64", "TMP64b"], writes=[("DT", ti)])
                P.op("vector", lambda e, r=r, sl=sl: e.tensor_tensor(out=DA[:r, sl], in0=DT[:r, sl], in1=aneg[:r, :], op=ALU.mult),
                     reads=[("DT", ti), ("ssdc", lj)], writes=[("DA", ti)])
                b = nb()
                P.op("tensor", lambda e, b=b, r=r, sl=sl: e.matmul(banks[b][:r, 0:NH], lhsT=tri32[:r, :r], rhs=DA[:r, sl], start=True, stop=True),
                     reads=[("DA", ti), "CST"], writes=[("pb", b)], banks=[b])
                P.op("tensor", lambda e, b=b, r=r, sl=sl: e.matmul(banks[b][:, NH:2 * NH], lhsT=ones32[:r, :], rhs=DA[:r, sl], start=True, stop=True),
                     reads=[("DA", ti), "CST"], writes=[("pb", b)], banks=[b])
                P.op("vector", lambda e, b=b, r=r, sl=sl: e.tensor_copy(out=CS[:r, sl], in_=banks[b][:r, 0:NH]),
                     reads=[("pb", b)], writes=[("CS", ti)], banks=[b])
                P.op("vector", lambda e, b=b, r=r, sl=sl: e.tensor_scalar(out=NCS[:r, sl], in0=banks[b][:r, 0:NH], scalar1=-1.0, scalar2=None, op0=ALU.mult),
                     reads=[("pb", b)], writes=[("NCS", ti)], banks=[b])
                P.op("scalar", lambda e, b=b, r=r, sl=sl: e.activation(out=ECS[:r, sl], in_=banks[b][:r, 0:NH], func=AF.Exp),
                     reads=[("pb", b)], writes=[("ECS", ti)], banks=[b])
                P.op("scalar", lambda e, b=b, sl=sl: e.activation(out=CDEC[:, sl], in_=banks[b][:, NH:2 * NH], func=AF.Exp),
                     reads=[("pb", b)], writes=[("CDEC", ti)], banks=[b])
                P.op("vector", lambda e, b=b, r=r, sl=sl: e.tensor_tensor(out=DST[:r, sl], in0=banks[b][:r, NH:2 * NH], in1=CS[:r, sl], op=ALU.subtract),
                     reads=[("pb", b), ("CS", ti)], writes=[("DST", ti)], banks=[b])
                P.op("scalar", lambda e, r=r, sl=sl: e.activation(out=DST[:r, sl], in_=DST[:r, sl], func=AF.Exp),
                     reads=[("DST", ti)], writes=[("DST", ti)])
            if stage <= 2:
                return
            for g in range(NG if stage > 3 else 1):
                strow = (lj * NG + g) * 128
                if pidx == 0:
                    P.op("vector", lambda e: e.memset(SST, 0.0), writes=["SST"])
                else:
                    P.dma("sync", SST, st_d[strow:strow + 128, :], reads=[("std", lj, g)], writes=["SST"])
                P.op("scalar", lambda e: e.copy(out=SBF, in_=SST), reads=["SST"], writes=["SBF"])
                nwg, nwgtok = nwg_rot.next()
                P.dma("sync", nwg, snw_d[lj:lj + 1, g * 512:(g + 1) * 512].partition_broadcast(128), writes=[nwgtok])
                for hb in range(2):
                    wz, wztok = wload(winv[:, :, g * 512 + hb * 256: g * 512 + (hb + 1) * 256], (16, 256))
                    for ti, (row0, r) in enumerate(part):
                        b = nb()
                        for k in range(16):
                            P.op("tensor", lambda e, b=b, k=k, wz=wz, c0=cols[ti], r=r: e.matmul(
                                banks[b][:r, 0:256], lhsT=UT[:, k, c0:c0 + r], rhs=wz[:, k, :], start=(k == 0), stop=(k == 15)),
                                reads=[wztok, ("UT", ti)], writes=[("pb", b)], banks=[b])
                        P.op("scalar", lambda e, b=b, ti=ti, r=r, hb=hb: e.activation(
                            out=ZS[:r, ti * 512 + hb * 256: ti * 512 + (hb + 1) * 256], in_=banks[b][:r, 0:256], func=AF.Silu),
                            reads=[("pb", b)], writes=[("ZS", ti)], banks=[b])
                chunk_list = [(DI + g * 512 + c * 128, 4 * g + c, XT[:, c * SMAX:(c + 1) * SMAX], ("XT", c)) for c in range(4)]
                chunk_list.append((DI + DI + g * 128, 32 + g, BT, "BT"))
                chunk_list.append((DI + DI + 1024 + g * 128, 40 + g, CT, "CT"))
                wblocks = {}
                for (colw, cch, dst, dtok) in chunk_list:
                    blk = colw // 256
                    if blk not in wblocks:
                        wblocks[blk] = wload(winv[:, :, blk * 256:(blk + 1) * 256], (16, 256))
                    wx, wxtok = wblocks[blk]
                    o_in = colw - blk * 256
                    xp, xptok = xpre_rot.next()
                    hal = convhalo[:, (lj * 48 + cch) * 3:(lj * 48 + cch) * 3 + 3]
                    P.op("vector", lambda e, xp=xp, hal=hal: e.tensor_copy(out=xp[:, 0:3], in_=hal), reads=[("chalo", lj, cch)], writes=[xptok])
                    for (n0, n1) in tokgroups(S):
                        b = nb()
                        for k in range(16):
                            P.op("tensor", lambda e, b=b, k=k, wx=wx, o_in=o_in, n0=n0, n1=n1: e.matmul(
                                banks[b][:, 0:n1 - n0], lhsT=wx[:, k, o_in:o_in + 128], rhs=UT[:, k, n0:n1], start=(k == 0), stop=(k == 15)),
                                reads=[wxtok] + allut, writes=[("pb", b)], banks=[b])
                        copy_op(evac_eng(), xp[:, 3 + n0:3 + n1], banks[b][:, 0:n1 - n0], reads=[("pb", b)], writes=[xptok], banks=[b])
                    P.op("vector", lambda e, xp=xp, hal=hal: e.tensor_copy(out=hal, in_=xp[:, S:S + 3]), reads=[xptok], writes=[("chalo", lj, cch)])
                    ca, catok = cacc_rot.next()
                    cwb = (lj * 48 + cch) * 4
                    P.op("vector", lambda e, ca=ca, xp=xp, cwb=cwb, cch=cch: e.tensor_scalar(
                        out=ca[:, 0:S], in0=xp[:, 0:S], scalar1=cwt[:, cwb:cwb + 1], scalar2=cbt[:, lj * 48 + cch:lj * 48 + cch + 1],
                        op0=ALU.mult, op1=ALU.add), reads=[xptok, "cwt", "cbt"], writes=[catok])
                    for kq in range(1, 4):
                        P.op("vector", lambda e, ca=ca, xp=xp, cwb=cwb, kq=kq: e.scalar_tensor_tensor(
                            out=ca[:, 0:S], in0=xp[:, kq:kq + S], scalar=cwt[:, cwb + kq:cwb + kq + 1], in1=ca[:, 0:S],
                            op0=ALU.mult, op1=ALU.add), reads=[xptok, catok, "cwt"], writes=[catok])
                    P.op("scalar", lambda e, ca=ca, dst=dst: e.activation(out=dst[:, 0:S], in_=ca[:, 0:S], func=AF.Silu),
                         reads=[catok], writes=[dtok])
                for ti, (row0, r) in enumerate(part):
                    c0 = cols[ti]
                    sl = slice(ti * NH, (ti + 1) * NH)
                    hs = slice(ti * NH + g * 8, ti * NH + g * 8 + 8)
                    b = nb()
                    pbv = banks[b][:].bitcast(BF16)
                    for c in range(4):
                        P.op("tensor", lambda e, pbv=pbv, c=c, c0=c0, r=r: e.transpose(
                            out=pbv[:r, c * 128:(c + 1) * 128], in_=XT[:, c * SMAX + c0:c * SMAX + c0 + r], identity=identb[:, :]),
                            reads=[("XT", c), "identb"], writes=[("pb", b)], banks=[b])
                    P.op("tensor", lambda e, pbv=pbv, c0=c0, r=r: e.transpose(
                        out=pbv[:r, 512:640], in_=BT[:, c0:c0 + r], identity=identb[:, :]),
                        reads=["BT", "identb"], writes=[("pb", b)], banks=[b])
                    xtk, xtktok = xtok_rot.next()
                    xdt, xdttok = xdt_rot.next()
                    xds, xdstok = xds_rot.next()
                    btk, btktok = btok_rot.next()
                    P.op("scalar", lambda e, xtk=xtk, pbv=pbv, r=r: e.copy(out=xtk[:r, :], in_=pbv[:r, 0:512]),
                         reads=[("pb", b)], writes=[xtktok], banks=[b])
                    P.op("scalar", lambda e, btk=btk, pbv=pbv, r=r: e.copy(out=btk[:r, :], in_=pbv[:r, 512:640]),
                         reads=[("pb", b)], writes=[btktok], banks=[b])
                    P.op("vector", lambda e, xdt=xdt, xtk=xtk, r=r, hs=hs: e.tensor_tensor(
                        out=xdt[:r, :].rearrange("p (h d) -> p h d", h=8), in0=xtk[:r, :].rearrange("p (h d) -> p h d", h=8),
                        in1=DT[:r, hs].unsqueeze(2).to_broadcast([r, 8, HD]), op=ALU.mult),
                        reads=[xtktok, ("DT", ti)], writes=[xdttok])
                    P.op("vector", lambda e, xds=xds, xdt=xdt, r=r, hs=hs: e.tensor_tensor(
                        out=xds[:r, :].rearrange("p (h d) -> p h d", h=8), in0=xdt[:r, :].rearrange("p (h d) -> p h d", h=8),
                        in1=DST[:r, hs].unsqueeze(2).to_broadcast([r, 8, HD]), op=ALU.mult),
                        reads=[xdttok, ("DST", ti)], writes=[xdstok])
                    b = nb()
                    P.op("tensor", lambda e, b=b, c0=c0, r=r: e.matmul(banks[b][:r, 0:r], lhsT=BT[:, c0:c0 + r], rhs=CT[:, c0:c0 + r], start=True, stop=True),
                         reads=["BT", "CT"], writes=[("pb", b)], banks=[b])
                    cbt_, cbttok = cbt_rot.next()
                    P.op("vector", lambda e, cbt_=cbt_, b=b, r=r: e.tensor_copy(out=cbt_[:r, 0:r], in_=banks[b][:r, 0:r]),
                         reads=[("pb", b)], writes=[cbttok], banks=[b])
                    byo = nb()
                    P.op("tensor", lambda e, byo=byo, c0=c0, r=r: e.matmul(banks[byo][:r, :], lhsT=CT[:, c0:c0 + r], rhs=SBF, start=True, stop=True),
                         reads=["CT", "SBF"], writes=[("pb", byo)], banks=[byo])
                    t1, t1tok = t1_rot.next()
                    P.op("vector", lambda e, t1=t1, byo=byo, r=r, hs=hs: e.tensor_tensor(
                        out=t1[:r, :].rearrange("p (h d) -> p h d", h=8), in0=banks[byo][:r, :].rearrange("p (h d) -> p h d", h=8),
                        in1=ECS[:r, hs].unsqueeze(2).to_broadcast([r, 8, HD]), op=ALU.mult),
                        reads=[("pb", byo), ("ECS", ti)], writes=[t1tok], banks=[byo])
                    by = nb()
                    mts = []
                    for half in range(2):
                        bs = nb()
                        for hh in range(4):
                            h = half * 4 + hh
                            hcol = ti * NH + g * 8 + h
                            P.op("tensor", lambda e, bs=bs, hh=hh, hcol=hcol, r=r: e.matmul(
                                banks[bs][:r, hh * 128:hh * 128 + r], lhsT=DA[:r, hcol:hcol + 1].to_broadcast([r, r]), rhs=tri32[:r, :r],
                                start=True, stop=False), reads=[("DA", ti), "CST"], writes=[("pb", bs)], banks=[bs])
                            P.op("tensor", lambda e, bs=bs, hh=hh, r=r: e.matmul(
                                banks[bs][:r, hh * 128:hh * 128 + r], lhsT=identb[:r, :r], rhs=maskb[:r, :r],
                                start=False, stop=True), reads=["identb", "maskb"], writes=[("pb", bs)], banks=[bs])
                        for hh in range(4):
                            h = half * 4 + hh
                            hcol = ti * NH + g * 8 + h
                            ee, eetok = ee_rot.next()
                            mt, mttok = mt_rot.next()
                            P.op("scalar", lambda e, ee=ee, bs=bs, hh=hh, hcol=hcol, r=r: e.activation(
                                out=ee[:r, 0:r], in_=banks[bs][:r, hh * 128:hh * 128 + r], func=AF.Exp, bias=NCS[:r, hcol:hcol + 1]),
                                reads=[("pb", bs), ("NCS", ti)], writes=[eetok], banks=[bs])
                            P.op("vector", lambda e, mt=mt, ee=ee, cbt_=cbt_, r=r: e.tensor_tensor(
                                out=mt[:r, 0:r], in0=ee[:r, 0:r], in1=cbt_[:r, 0:r], op=ALU.mult),
                                reads=[eetok, cbttok], writes=[mttok])
                            P.op("tensor", lambda e, by=by, mt=mt, xdt=xdt, h=h, r=r: e.matmul(
                                banks[by][:r, h * HD:(h + 1) * HD], lhsT=mt[:r, 0:r], rhs=xdt[:r, h * HD:(h + 1) * HD], start=True, stop=True),
                                reads=[mttok, xdttok], writes=[("pb", by)], banks=[by])
                    P.op("vector", lambda e, t1=t1, by=by, r=r: e.tensor_tensor(out=t1[:r, :], in0=t1[:r, :], in1=banks[by][:r, :], op=ALU.add),
                         reads=[t1tok, ("pb", by)], writes=[t1tok], banks=[by])
                    t2, t2tok = t2_rot.next()
                    P.op("vector", lambda e, t2=t2, xtk=xtk, r=r, g=g: e.tensor_tensor(
                        out=t2[:r, :].rearrange("p (h d) -> p h d", h=8), in0=xtk[:r, :].rearrange("p (h d) -> p h d", h=8),
                        in1=dskip[:r, g * 8:g * 8 + 8].unsqueeze(2).to_broadcast([r, 8, HD]), op=ALU.mult),
                        reads=[xtktok, ("ssdc", lj)], writes=[t2tok])
                    P.op("vector", lambda e, t1=t1, t2=t2, r=r: e.tensor_tensor(out=t1[:r, :], in0=t1[:r, :], in1=t2[:r, :], op=ALU.add),
                         reads=[t1tok, t2tok], writes=[t1tok])
                    P.op("vector", lambda e, t1=t1, ti=ti, r=r: e.tensor_tensor(out=t1[:r, :], in0=t1[:r, :], in1=ZS[:r, ti * 512:(ti + 1) * 512], op=ALU.mult),
                         reads=[t1tok, ("ZS", ti)], writes=[t1tok])
                    scol = 5 + (ti % 2)
                    yn, yntok = yn_rot.next()
                    sumsq(t1[:r, :], r, scol, [t1tok], yn[:r, :], [yntok])
                    rstd(r, scol, 512)
                    P.op("vector", lambda e, yn=yn, t1=t1, r=r, scol=scol, nwg=nwg: e.scalar_tensor_tensor(
                        out=yn[:r, :], in0=t1[:r, :], scalar=rsd[:r, scol:scol + 1], in1=nwg[:r, :], op0=ALU.mult, op1=ALU.mult),
                        reads=[t1tok, ("rsd", scol), nwgtok], writes=[yntok])
                    b = nb()
                    pbv2 = banks[b][:].bitcast(BF16)
                    for c in range(4):
                        P.op("tensor", lambda e, pbv2=pbv2, c=c, yn=yn, r=r: e.transpose(
                            out=pbv2[:, c * 128:c * 128 + r], in_=yn[:r, c * 128:(c + 1) * 128], identity=identb[:r, :r]),
                            reads=[yntok, "identb"], writes=[("pb", b)], banks=[b])
                    copy_op(evac_eng(), YT[:, g * 4:g * 4 + 4, c0:c0 + r], pbv2[:, 0:512].rearrange("p (k t) -> p k t", k=4)[:, :, 0:r],
                            reads=[("pb", b)], writes=[("YT", g, ti)], banks=[b])
                    b = nb()
                    P.op("tensor", lambda e, b=b, btk=btk, xds=xds, r=r: e.matmul(banks[b][:, :], lhsT=btk[:r, :], rhs=xds[:r, :], start=True, stop=True),
                         reads=[btktok, xdstok], writes=[("pb", b)], banks=[b])
                    P.op("vector", lambda e, hs=hs: e.tensor_tensor(
                        out=SST.rearrange("p (h d) -> p h d", h=8), in0=SST.rearrange("p (h d) -> p h d", h=8),
                        in1=CDEC[:, hs].unsqueeze(2).to_broadcast([128, 8, HD]), op=ALU.mult),
                        reads=["SST", ("CDEC", ti)], writes=["SST"])
                    P.op("vector", lambda e, b=b: e.tensor_tensor(out=SST, in0=SST, in1=banks[b][:, :], op=ALU.add),
                         reads=["SST", ("pb", b)], writes=["SST"], banks=[b])
                    P.op("scalar", lambda e: e.copy(out=SBF, in_=SST), reads=["SST"], writes=["SBF"])
                if not last_part:
                    P.dma("sync", st_d[strow:strow + 128, :], SST, reads=["SST"], writes=[("std", lj, g)])
            P.barrier()
            if stage <= 4:
                return
            woutv = wout_d[lj].rearrange("(k p) n -> p k n", p=128)
            for nblk in range(4):
                accs = [nb() for _ in part]
                for pc in range(4):
                    wo, wotok = wload(woutv[:, pc * 8:pc * 8 + 8, nblk * 512:(nblk + 1) * 512], (8, 512))
                    for kk in range(8):
                        ch = pc * 8 + kk
                        for ti, (row0, r) in enumerate(part):
                            P.op("tensor", lambda e, a=accs[ti], ch=ch, kk=kk, wo=wo, c0=cols[ti], r=r: e.matmul(
                                banks[a][:r, :], lhsT=YT[:, ch, c0:c0 + r], rhs=wo[:, kk, :], start=(ch == 0), stop=(ch == 31)),
                                reads=[wotok, ("YT", ch // 4, ti)], writes=[("pb", accs[ti])], banks=[accs[ti]])
                for ti, (row0, r) in enumerate(part):
                    copy_op(evac_eng(), FB[:r, ti * D + nblk * 512: ti * D + (nblk + 1) * 512], banks[accs[ti]][:r, :],
                            reads=[("pb", accs[ti])], writes=[("FB", ti)], banks=[accs[ti]])
            postnorm(part, li, 1)

        for pidx, part in enumerate(parts):
            last_part = pidx == len(parts) - 1
            for ti, (row0, r) in enumerate(part):
                src = meta_d[0:NMETA, :] if row0 == 0 else x_d[row0 - NMETA:row0 - NMETA + r, :]
                P.dma("sync", H[:r, ti, :], src, writes=[("H", ti)])
            for li in range(layers if stage > 0 else 0):
                if li % 2 == 0:
                    ssd(part, li, pidx, last_part)
                else:
                    pool(part, li, pidx)
                P.barrier()
                if stage > 5:
                    ffn(part, li)
                P.barrier()
            for ti, (row0, r) in enumerate(part):
                if row0 == 0:
                    continue
                P.dma("sync", y_d[row0 - NMETA:row0 - NMETA + r, :], H[:r, ti, :], reads=[("H", ti)], writes=[("y", pidx, ti)])
        P.op("sync", None, reads=[("y", pidx, ti) for pidx, part in enumerate(parts) for ti, (row0, r) in enumerate(part) if row0 != 0])
        P.emit(st)
    return nc


def host_consts():
    cst = np.zeros((128, 512), np.float32)
    cst[:, 0:128] = np.eye(128, dtype=np.float32)
    cst[:, 128:256] = np.triu(np.ones((128, 128), np.float32))
    cst[:, 256:384] = 1.0
    cst[:, 384:512] = np.tril(np.full((128, 128), -30000.0, np.float32), -1)
    pcnt = np.zeros((1, 64), np.float32)
    for g, win in enumerate((2, 4, 8, 16)):
        t = np.arange(16)
        pcnt[0, g * 16:(g + 1) * 16] = win / np.minimum(t + 1, win)
    return cst, pcnt


_NC_CACHE = {}


def kernel(x, meta_tokens, norm_w, ssd_w_in, ssd_conv_w, ssd_conv_b, ssd_dt_bias, ssd_a_log, ssd_d,
           ssd_norm_w, ssd_w_out, pool_w, pool_b, pool_scale, ffn_w_gate, ffn_w_up, ffn_w_down):
    f = lambda a: np.ascontiguousarray(np.asarray(a, dtype=np.float32))
    x = f(x)
    B, nx, _ = x.shape
    key = (nx, DEPTH)
    if key not in _NC_CACHE:
        _NC_CACHE[key] = build(nx, DEPTH)
    nc = _NC_CACHE[key]
    cst, pcnt = host_consts()
    cw = f(ssd_conv_w)
    cw = np.ascontiguousarray(cw.reshape(2, 4, 48, 128).transpose(0, 3, 2, 1).reshape(2, 128, 48 * 4))
    cb = f(ssd_conv_b)
    cb = np.ascontiguousarray(cb.reshape(2, 48, 128).transpose(0, 2, 1))
    shared = {
        "meta": f(meta_tokens), "norm_w": f(norm_w), "ssd_w_in": f(ssd_w_in), "ssd_w_out": f(ssd_w_out),
        "pool_w": f(pool_w), "ffn_w_gate": f(ffn_w_gate), "ffn_w_up": f(ffn_w_up), "ffn_w_down": f(ffn_w_down),
        "ssd_cw": cw, "ssd_cb": cb, "ssd_dt_bias": f(ssd_dt_bias), "ssd_a_log": f(ssd_a_log), "ssd_d": f(ssd_d),
        "ssd_norm_w": f(ssd_norm_w), "pool_b": f(pool_b), "pool_scale": f(pool_scale), "cst": cst, "pcnt": pcnt,
    }
    ncores = 8
    in_maps = []
    for c in range(ncores):
        m = dict(shared)
        m["x"] = x[c % B]
        in_maps.append(m)
    res = run_bass_kernel_spmd(nc, in_maps, core_ids=list(range(ncores)))
    out = np.stack([res.results[b]["y"] for b in range(B)], axis=0)
    return out.astype(np.float32)
```

```python
import numpy as np
from contextlib import ExitStack
import concourse.bass as bass
import concourse.mybir as mybir
from concourse.bass_utils import run_bass_kernel_spmd

F32 = mybir.dt.float32
BF16 = mybir.dt.bfloat16
AF = mybir.ActivationFunctionType
ALU = mybir.AluOpType

D = 2048
NMETA = 16
DI = 4096
NH = 64
HD = 64
NS = 128
NG = 8
FH = 5632
DINP = 10304
EPS = 1e-6
DEPTH = 4
ENGINES = ["sync", "scalar", "gpsimd", "vector", "tensor"]
BAR_ENGS = ["sync", "scalar", "vector", "tensor"]


class Prog:
    def __init__(self, nc, n_dma_sems=8):
        self.nc = nc
        self.ops = []
        self.n_dma_sems = n_dma_sems

    COST = {"tensor": 115.0, "scalar": 450.0, "vector": 500.0, "sync": 2500.0, "gpsimd": 6000.0}

    def op(self, eng, fn, reads=(), writes=(), dma=False, banks=(), barrier=False, cost=None):
        self.ops.append(dict(eng=eng, fn=fn, reads=tuple(reads), writes=tuple(writes),
                             dma=dma, banks=tuple(banks), barrier=barrier,
                             cost=(self.COST[eng] if cost is None else cost)))

    def schedule(self):
        import heapq
        ops = self.ops
        n = len(ops)
        last_w, readers, last_bank = {}, {}, {}
        preds = [None] * n
        prev_gp = None
        for i, o in enumerate(ops):
            d = set()
            for b in o["banks"]:
                if b in last_bank:
                    d.add(last_bank[b])
                last_bank[b] = i
            for t in o["reads"]:
                if t in last_w:
                    d.add(last_w[t])
            for t in o["writes"]:
                if t in last_w:
                    d.add(last_w[t])
                d.update(readers.get(t, ()))
            if o["eng"] == "gpsimd":
                if prev_gp is not None:
                    d.add(prev_gp)
                prev_gp = i
            d.discard(i)
            preds[i] = d
            for t in o["reads"]:
                readers.setdefault(t, []).append(i)
            for t in o["writes"]:
                last_w[t] = i
                readers[t] = []
        succs = [[] for _ in range(n)]
        indeg = [0] * n
        for i in range(n):
            indeg[i] = len(preds[i])
            for p in preds[i]:
                succs[p].append(i)
        seg = [0] * n
        sid = 0
        i = 0
        bar_groups = []
        while i < n:
            if ops[i]["barrier"]:
                j = i
                while j < n and ops[j]["barrier"]:
                    j += 1
                bar_groups.append(list(range(i, j)))
                sid += 1
                i = j
            else:
                seg[i] = sid
                i += 1
        nseg = sid + 1
        seg_ops = [[] for _ in range(nseg)]
        for i in range(n):
            if not ops[i]["barrier"]:
                seg_ops[seg[i]].append(i)
        finish = [0.0] * n
        done = [False] * n
        order = []
        eng_free = {e: 0.0 for e in ENGINES}
        LAT = 350.0
        pending_gp = []
        tnow = 0.0
        eligible = [False] * n
        ready_heap = {e: [] for e in ENGINES}
        future = {e: [] for e in ENGINES}
        rdy_time = [0.0] * n

        def make_ready(i):
            rt = 0.0
            e = ops[i]["eng"]
            for p in preds[i]:
                f = finish[p] + (LAT if ops[p]["eng"] != e or ops[p]["dma"] else 0.0)
                if f > rt:
                    rt = f
            rdy_time[i] = rt
            heapq.heappush(future[e], (rt, i))

        for s_ in range(nseg):
            members = seg_ops[s_]
            newly = [i for i in members if not eligible[i]]
            if s_ + 1 < nseg:
                newly += [i for i in seg_ops[s_ + 1] if ops[i]["eng"] == "gpsimd"]
            for i in newly:
                eligible[i] = True
            for i in newly:
                if indeg[i] == 0:
                    make_ready(i)
            remaining = sum(1 for i in members if not done[i])
            ev = []
            while remaining > 0:
                progressed = False
                tmin = None
                for e in ENGINES:
                    t = eng_free[e]
                    fut = future[e]
                    rh = ready_heap[e]
                    while fut and fut[0][0] <= t:
                        _, i = heapq.heappop(fut)
                        heapq.heappush(rh, i)
                    if not rh and fut:
                        t2 = fut[0][0]
                        cand = t2
                    elif rh:
                        cand = t
                    else:
                        cand = None
                    if cand is not None and (tmin is None or cand < tmin[0]):
                        tmin = (cand, e)
                if tmin is None:
                    raise RuntimeError("scheduler deadlock")
                t, e = tmin
                fut = future[e]
                rh = ready_heap[e]
                while fut and fut[0][0] <= t:
                    _, i = heapq.heappop(fut)
                    heapq.heappush(rh, i)
                i = heapq.heappop(rh)
                st_ = max(t, eng_free[e])
                fi = st_ + ops[i]["cost"]
                if ops[i]["dma"] and e != "gpsimd":
                    eng_free[e] = st_ + 60.0
                else:
                    eng_free[e] = fi
                finish[i] = fi
                done[i] = True
                order.append((st_, i))
                if seg[i] == s_:
                    remaining -= 1
                for q in succs[i]:
                    indeg[q] -= 1
                    if indeg[q] == 0 and eligible[q]:
                        make_ready(q)
            if s_ < len(bar_groups):
                tb = max(finish[i] for i in members) if members else start_t
                for bi in bar_groups[s_]:
                    finish[bi] = tb
                    order.append((tb + 1e-3, bi))
                    for q in succs[bi]:
                        indeg[q] -= 1
                for e in BAR_ENGS:
                    eng_free[e] = max(eng_free[e], tb) + 0.01
        order.sort(key=lambda x: (x[0], x[1]))
        perm = [i for _, i in order]
        assert len(perm) == n and len(set(perm)) == n
        self.ops = [ops[i] for i in perm]
        self.sim_time = max(finish) if n else 0.0

    def dma(self, eng, out, in_, reads=(), writes=()):
        self.op(eng, lambda e: e.dma_start(out=out, in_=in_), reads, writes, dma=True)

    def barrier(self):
        for e in BAR_ENGS:
            self.op(e, None, barrier=True)

    def emit(self, stack):
        nc = self.nc
        ops = self.ops
        n = len(ops)
        dma_count = {e: 0 for e in ENGINES}
        stream = [None] * n
        for i, o in enumerate(ops):
            if o["dma"]:
                k = dma_count[o["eng"]]
                dma_count[o["eng"]] += 1
                stream[i] = ("dma", o["eng"], k % self.n_dma_sems)
            else:
                stream[i] = o["eng"]
        last_w = {}
        readers = {}
        deps = [None] * n
        bank_last = {}
        last_on_stream = {}
        for i, o in enumerate(ops):
            d = set()
            if o["barrier"]:
                for s, j in last_on_stream.items():
                    if s == "gpsimd" or (isinstance(s, tuple) and s[1] == "gpsimd"):
                        continue
                    d.add(j)
            for b in o["banks"]:
                bl = bank_last.setdefault(b, {})
                for e2, j in bl.items():
                    if e2 != o["eng"]:
                        d.add(j)
                bl[o["eng"]] = i
            for t in o["reads"]:
                if t in last_w:
                    d.add(last_w[t])
            for t in o["writes"]:
                if t in last_w:
                    d.add(last_w[t])
                for r in readers.get(t, ()):
                    d.add(r)
            if o["dma"]:
                p = last_on_stream.get(stream[i])
                if p is not None:
                    d.add(p)
            d.discard(i)
            deps[i] = d
            for t in o["reads"]:
                readers.setdefault(t, []).append(i)
            for t in o["writes"]:
                last_w[t] = i
                readers[t] = []
            if not o["barrier"]:
                last_on_stream[stream[i]] = i
        needed = [bool(o["dma"]) for o in ops]
        red = [None] * n
        seen = {e: {} for e in ENGINES}
        for i, o in enumerate(ops):
            e = o["eng"]
            best = {}
            for d in deps[i]:
                s = stream[d]
                if s == "tensor" and e == "tensor" and not o["dma"]:
                    continue
                if d > best.get(s, -1):
                    best[s] = d
            lst = []
            for s, d in best.items():
                if seen[e].get(s, -1) >= d:
                    continue
                seen[e][s] = d
                lst.append(d)
                needed[d] = True
            red[i] = lst
        sems = {}
        for s in sorted(set(stream), key=str):
            nm = "s_" + ("_".join(str(x) for x in s) if isinstance(s, tuple) else s)
            sems[s] = stack.enter_context(nc.semaphore(nm))
        cnt = {s: 0 for s in sems}
        val = [0] * n
        for i, o in enumerate(ops):
            if needed[i]:
                s = stream[i]
                cnt[s] += 16 if o["dma"] else 1
                val[i] = cnt[s]
        self.max_sem = max(cnt.values()) if cnt else 0
        block = stack.enter_context(nc.Block())
        per_eng = {e: [] for e in ENGINES}
        for i, o in enumerate(ops):
            per_eng[o["eng"]].append(i)

        def make(e_name):
            def body(eng):
                for i in per_eng[e_name]:
                    o = ops[i]
                    for d in red[i]:
                        eng.wait_ge(sems[stream[d]], val[d])
                    if o["fn"] is None:
                        assert not needed[i]
                        continue
                    ins = o["fn"](eng)
                    if needed[i]:
                        ins.then_inc(sems[stream[i]], 16 if o["dma"] else 1)
            return body

        for e_name in ENGINES:
            if per_eng[e_name]:
                getattr(block, e_name)(make(e_name))


class Rot:
    def __init__(self, items):
        self.items = items
        self.i = 0

    def next(self):
        it = self.items[self.i % len(self.items)]
        self.i += 1
        return it


def make_parts(nx):
    tiles = [(0, NMETA)] + [(NMETA + 128 * i, 128) for i in range(nx // 128)]
    parts = [tiles[0:5]]
    k = 5
    while k < len(tiles):
        parts.append(tiles[k:k + 4])
        k += 4
    return parts


def build(nx=2048, layers=DEPTH, dbg=None):
    nc = bass.Bass("TRN2", target_bir_lowering=False)
    din = lambda name, shape: nc.dram_tensor(name, list(shape), F32, kind="ExternalInput").ap()
    x_d = din("x", [nx, D])
    meta_d = din("meta", [NMETA, D])
    normw_d = din("norm_w", [DEPTH, 4, D])
    win_d = din("ssd_w_in", [2, D, DINP])
    wout_d = din("ssd_w_out", [2, DI, D])
    poolw_d = din("pool_w", [2, 4, 512, 512])
    wg_d = din("ffn_w_gate", [DEPTH, D, FH])
    wu_d = din("ffn_w_up", [DEPTH, D, FH])
    wd_d = din("ffn_w_down", [DEPTH, FH, D])
    cw_d = din("ssd_cw", [2, 128, 48 * 4])
    cb_d = din("ssd_cb", [2, 128, 48])
    dtb_d = din("ssd_dt_bias", [2, NH])
    alog_d = din("ssd_a_log", [2, NH])
    dsk_d = din("ssd_d", [2, NH])
    snw_d = din("ssd_norm_w", [2, DI])
    pb_d = din("pool_b", [2, D])
    psc_d = din("pool_scale", [2, D])
    cst_d = din("cst", [128, 512])
    pcnt_d = din("pcnt", [1, 64])
    y_d = nc.dram_tensor("y", [nx, D], F32, kind="ExternalOutput").ap()
    st_d = nc.dram_tensor("ssd_state", [2 * NG * 128, 512], F32, kind="Internal").ap()
    stage = (dbg or {}).get("stage", 99)

    parts = make_parts(nx)
    SMAX = 528
    TMAX = 5

    with ExitStack() as st:
        sb = lambda name, shape, dt: st.enter_context(nc.sbuf_tensor(name, shape, dt))
        P = Prog(nc)
        H = sb("H", [128, TMAX, D], F32)
        FB = sb("FB", [128, TMAX * D], F32)
        UT = sb("UT", [128, 16, SMAX], BF16)
        BIG = sb("BIG", [128, 44 * SMAX], BF16)
        WBs = [sb(f"WB{i}", [128, 4096], BF16) for i in range(4)]
        WBC = [sb(f"WBC{i}", [128, D], F32) for i in range(2)]
        CST = sb("CST", [128, 512], F32)
        identb = sb("identb", [128, 128], BF16)
        maskb = sb("maskb", [128, 128], BF16)
        ssq = sb("ssq", [128, 8], F32)
        rsd = sb("rsd", [128, 8], F32)
        sgt = [sb(f"sgt{i}", [128, 512], BF16) for i in range(2)]
        convhalo = sb("convhalo", [128, 2 * 48 * 3], F32)
        poolhalo = sb("poolhalo", [128, 2 * 16 * 15], F32)
        ssdc = sb("ssdc", [128, 2 * 3 * NH], F32)
        cwt = sb("cwt", [128, 2 * 48 * 4], F32)
        cbt = sb("cbt", [128, 2 * 48], F32)
        pcn = sb("pcn", [128, 64], F32)
        banks = [st.enter_context(nc.psum_tensor(f"pb{i}", [128, 512], F32)) for i in range(8)]
        ident32 = CST[:, 0:128]
        tri32 = CST[:, 128:256]
        ones32 = CST[:, 256:384]
        mask32 = CST[:, 384:512]

        bank_i = [0]

        def nb():
            b = bank_i[0] % 8
            bank_i[0] += 1
            return b

        wb_rot = Rot([(WBs[i], ("WB", i)) for i in range(4)])
        wbc_rot = Rot([(WBC[i], ("WBC", i)) for i in range(2)])
        FBb = FB[:].bitcast(BF16)
        utok_rot = Rot([(FBb[:, (3 + i) * 2 * D:(3 + i) * 2 * D + D], ("FB", 3 + i)) for i in range(2)])
        sgt_rot = Rot([(sgt[i], ("sgt", i)) for i in range(2)])
        evac_i = [0]

        def evac_eng():
            evac_i[0] += 1
            return "vector" if evac_i[0] % 2 else "scalar"

        def copy_op(eng, out, in_, reads, writes, banks=()):
            if eng == "vector":
                P.op("vector", lambda e: e.tensor_copy(out=out, in_=in_), reads, writes, banks=banks)
            else:
                P.op("scalar", lambda e: e.copy(out=out, in_=in_), reads, writes, banks=banks)

        P.dma("sync", CST[:], cst_d, writes=["CST"])
        P.op("vector", lambda e: e.tensor_copy(out=identb[:], in_=ident32), reads=["CST"], writes=["identb"])
        P.op("vector", lambda e: e.tensor_copy(out=maskb[:], in_=mask32), reads=["CST"], writes=["maskb"])
        P.dma("sync", cwt[:].rearrange("p (l c) -> p l c", l=2), cw_d.rearrange("l p c -> p l c"), writes=["cwt"])
        P.dma("sync", cbt[:].rearrange("p (l c) -> p l c", l=2), cb_d.rearrange("l p c -> p l c"), writes=["cbt"])
        for l in range(2):
            P.dma("sync", ssdc[:, (l * 3 + 0) * NH:(l * 3 + 1) * NH], alog_d[l:l + 1, :].partition_broadcast(128), writes=[("ssdc", l)])
            P.dma("sync", ssdc[:, (l * 3 + 1) * NH:(l * 3 + 2) * NH], dsk_d[l:l + 1, :].partition_broadcast(128), writes=[("ssdc", l)])
            P.dma("sync", ssdc[:, (l * 3 + 2) * NH:(l * 3 + 3) * NH], dtb_d[l:l + 1, :].partition_broadcast(128), writes=[("ssdc", l)])
            an = ssdc[:, (l * 3) * NH:(l * 3 + 1) * NH]
            P.op("scalar", lambda e, an=an: e.activation(out=an, in_=an, func=AF.Exp), reads=[("ssdc", l)], writes=[("ssdc", l)])
            P.op("vector", lambda e, an=an: e.tensor_scalar(out=an, in0=an, scalar1=-1.0, scalar2=None, op0=ALU.mult), reads=[("ssdc", l)], writes=[("ssdc", l)])
        P.dma("sync", pcn[:], pcnt_d.partition_broadcast(128), writes=["pcn"])
        P.op("vector", lambda e: e.memset(convhalo[:], 0.0), writes=[("chalo", l, c) for l in range(2) for c in range(48)])
        P.op("vector", lambda e: e.memset(poolhalo[:], 0.0), writes=["poolhalo"])
        P.op("vector", lambda e: e.memset(ssq[:], 0.0), writes=["ssq"])

        def sumsq(src_ap, r, col, reads, junk, junk_toks):
            P.op("scalar", lambda e: e.activation(out=junk, in_=src_ap, func=AF.Square,
                                                  accum_out=ssq[:r, col:col + 1]),
                 reads=reads + ["ssq"], writes=list(junk_toks) + [("ssq", col)],
                 cost=230.0 + float(np.prod(src_ap.shape[1:])) / 1.2)

        def rstd(r, col, n):
            P.op("vector", lambda e: e.tensor_scalar(out=rsd[:r, col:col + 1], in0=ssq[:r, col:col + 1], scalar1=1.0 / n,
                                                     scalar2=EPS, op0=ALU.mult, op1=ALU.add),
                 reads=[("ssq", col)], writes=[("rsd", col)])
            P.op("scalar", lambda e: e.activation(out=rsd[:r, col:col + 1], in_=rsd[:r, col:col + 1], func=AF.Sqrt),
                 reads=[("rsd", col)], writes=[("rsd", col)])
            P.op("vector", lambda e: e.reciprocal(out=rsd[:r, col:col + 1], in_=rsd[:r, col:col + 1]),
                 reads=[("rsd", col)], writes=[("rsd", col)])

        def load_wbc(row_ap):
            w, tok = wbc_rot.next()
            P.dma("sync", w[:], row_ap.partition_broadcast(128), writes=[tok])
            return w, tok

        def prenorm(part, li, j, fp32_dst=None):
            w, wtok = load_wbc(normw_d[li, j:j + 1, :])
            col = 0
            for ti, (row0, r) in enumerate(part):
                sumsq(H[:r, ti, :], r, ti, [("H", ti)], FBb[:r, 2 * 2 * D:2 * 2 * D + D], [("FB", 2)])
                rstd(r, ti, D)
                if fp32_dst is None:
                    ut, uttok = utok_rot.next()
                    P.op("vector", lambda e, ut=ut, ti=ti, r=r: e.scalar_tensor_tensor(
                        out=ut[:r, :], in0=H[:r, ti, :], scalar=rsd[:r, ti:ti + 1], in1=w[:r, :], op0=ALU.mult, op1=ALU.mult),
                        reads=[("H", ti), ("rsd", ti), wtok], writes=[uttok], cost=2300.0)
                    for half in range(2):
                        b = nb()
                        pbv = banks[b][:].bitcast(BF16)
                        for kk in range(8):
                            k = half * 8 + kk
                            P.op("tensor", lambda e, pbv=pbv, kk=kk, k=k, ut=ut, r=r: e.transpose(
                                out=pbv[:, kk * 128:kk * 128 + r], in_=ut[:r, k * 128:(k + 1) * 128], identity=identb[:r, :r]),
                                reads=[uttok, "identb"], writes=[("pb", b)], banks=[b], cost=90.0)
                        copy_op(evac_eng(), UT[:, half * 8:half * 8 + 8, col:col + r],
                                pbv[:, 0:1024].rearrange("p (k t) -> p k t", k=8)[:, :, 0:r],
                                reads=[("pb", b)], writes=[("UT", ti)], banks=[b])
                else:
                    u32 = FB[:, 0:D]
                    P.op("vector", lambda e, ti=ti, r=r: e.scalar_tensor_tensor(
                        out=u32[:r, :], in0=H[:r, ti, :], scalar=rsd[:r, ti:ti + 1], in1=w[:r, :], op0=ALU.mult, op1=ALU.mult),
                        reads=[("H", ti), ("rsd", ti), wtok], writes=["u32"], cost=2300.0)
                    for q in range(4):
                        b = nb()
                        for kk in range(4):
                            k = q * 4 + kk
                            P.op("tensor", lambda e, b=b, kk=kk, k=k, r=r: e.transpose(
                                out=banks[b][:, kk * 128:kk * 128 + r], in_=u32[:r, k * 128:(k + 1) * 128], identity=ident32[:r, :r]),
                                reads=["u32", "CST"], writes=[("pb", b)], banks=[b], cost=90.0)
                        copy_op(evac_eng(), fp32_dst[:, q * 4:q * 4 + 4, 15 + col:15 + col + r],
                                banks[b][:].rearrange("p (k t) -> p k t", k=4)[:, :, 0:r],
                                reads=[("pb", b)], writes=[("UT32", ti)], banks=[b])
                col += r

        def postnorm(part, li, j):
            w, wtok = load_wbc(normw_d[li, j:j + 1, :])
            for ti, (row0, r) in enumerate(part):
                fb = FB[:r, ti * D:(ti + 1) * D]
                sumsq(fb.rearrange("p (a b) -> p a b", a=4), r, ti, [("FB", ti)], UT[:r, 0:4, 0:512],
                      [("UT", q) for q in range(TMAX)] + [("MIX", q) for q in range(16)])
                rstd(r, ti, D)
                P.op("vector", lambda e, fb=fb, ti=ti, r=r: e.scalar_tensor_tensor(
                    out=fb, in0=fb, scalar=rsd[:r, ti:ti + 1], in1=w[:r, :], op0=ALU.mult, op1=ALU.mult),
                    reads=[("FB", ti), ("rsd", ti), wtok], writes=[("FB", ti)], cost=2300.0)
                P.op("vector", lambda e, fb=fb, ti=ti, r=r: e.tensor_tensor(out=H[:r, ti, :], in0=H[:r, ti, :], in1=fb, op=ALU.add),
                     reads=[("FB", ti), ("H", ti)], writes=[("H", ti)], cost=2300.0)

        def tokgroups(S):
            if S <= 512:
                return [(0, S)]
            h = (S // 2 + 7) // 8 * 8
            return [(0, h), (h, S)]

        def ut_reads(part, n0, n1):
            res = []
            col = 0
            for ti, (row0, r) in enumerate(part):
                if col < n1 and col + r > n0:
                    res.append(("UT", ti))
                col += r
            return res

        def wload(src_ap, shape3):
            w, tok = wb_rot.next()
            a, b_ = shape3
            view = w[:, 0:a * b_].rearrange("p (a b) -> p a b", a=a)
            P.dma("gpsimd", view, src_ap, writes=[tok])
            return view, tok

        def ffn(part, li):
            S = sum(r for _, r in part)
            prenorm(part, li, 2)
            H1T = BIG[:, 0:44 * SMAX].rearrange("p (c t) -> p c t", c=44)
            wgv = wg_d[li].rearrange("(k p) n -> p k n", p=128)
            wuv = wu_d[li].rearrange("(k p) n -> p k n", p=128)
            for jb in range(FH // 256):
                wg, wgtok = wload(wgv[:, :, jb * 256:(jb + 1) * 256], (16, 256))
                wu, wutok = wload(wuv[:, :, jb * 256:(jb + 1) * 256], (16, 256))
                for c in range(2):
                    ch = jb * 2 + c
                    for (n0, n1) in tokgroups(S):
                        ba, bb = nb(), nb()
                        ur = ut_reads(part, n0, n1)
                        for k in range(16):
                            P.op("tensor", lambda e, ba=ba, k=k, c=c, wg=wg, n0=n0, n1=n1: e.matmul(
                                banks[ba][:, 0:n1 - n0], lhsT=wg[:, k, c * 128:(c + 1) * 128], rhs=UT[:, k, n0:n1],
                                start=(k == 0), stop=(k == 15)), reads=[wgtok] + ur, writes=[("pb", ba)], banks=[ba])
                        for k in range(16):
                            P.op("tensor", lambda e, bb=bb, k=k, c=c, wu=wu, n0=n0, n1=n1: e.matmul(
                                banks[bb][:, 0:n1 - n0], lhsT=wu[:, k, c * 128:(c + 1) * 128], rhs=UT[:, k, n0:n1],
                                start=(k == 0), stop=(k == 15)), reads=[wutok] + ur, writes=[("pb", bb)], banks=[bb])
                        sg, sgtok = sgt_rot.next()
                        P.op("scalar", lambda e, sg=sg, ba=ba, n0=n0, n1=n1: e.activation(
                            out=sg[:, 0:n1 - n0], in_=banks[ba][:, 0:n1 - n0], func=AF.Silu),
                            reads=[("pb", ba)], writes=[sgtok], banks=[ba])
                        P.op("vector", lambda e, sg=sg, bb=bb, ch=ch, n0=n0, n1=n1: e.tensor_tensor(
                            out=H1T[:, ch, n0:n1], in0=sg[:, 0:n1 - n0], in1=banks[bb][:, 0:n1 - n0], op=ALU.mult),
                            reads=[sgtok, ("pb", bb)], writes=[("H1T", ch)], banks=[bb])
            wdv = wd_d[li].rearrange("(k p) n -> p k n", p=128)
            for nblk in range(4):
                accs = [nb() for _ in part]
                for pc in range(6):
                    k0 = pc * 8
                    kn = min(8, 44 - k0)
                    wd, wdtok = wload(wdv[:, k0:k0 + kn, nblk * 512:(nblk + 1) * 512], (kn, 512))
                    for kk in range(kn):
                        ch = k0 + kk
                        col = 0
                        for ti, (row0, r) in enumerate(part):
                            P.op("tensor", lambda e, a=accs[ti], ch=ch, kk=kk, wd=wd, col=col, r=r: e.matmul(
                                banks[a][:r, :], lhsT=H1T[:, ch, col:col + r], rhs=wd[:, kk, :],
                                start=(ch == 0), stop=(ch == 43)), reads=[wdtok, ("H1T", ch)], writes=[("pb", accs[ti])], banks=[accs[ti]], cost=215.0)
                            col += r
                for ti, (row0, r) in enumerate(part):
                    copy_op(evac_eng(), FB[:r, ti * D + nblk * 512: ti * D + (nblk + 1) * 512], banks[accs[ti]][:r, :],
                            reads=[("pb", accs[ti])], writes=[("FB", ti)], banks=[accs[ti]])
            postnorm(part, li, 3)

        def pool(part, li, pidx):
            lj = li // 2
            S = sum(r for _, r in part)
            U32 = BIG[:].bitcast(F32)[:, 0:16 * (15 + SMAX)].rearrange("p (c t) -> p c t", c=16)
            halo = poolhalo[:, lj * 240:(lj + 1) * 240].rearrange("p (c t) -> p c t", c=16)
            P.op("vector", lambda e: e.tensor_copy(out=U32[:, :, 0:15], in_=halo), reads=["poolhalo"], writes=["U32h"])
            prenorm(part, li, 0, fp32_dst=U32)
            allut = [("UT32", ti) for ti in range(len(part))]
            P.op("vector", lambda e: e.tensor_copy(out=halo, in_=U32[:, :, S:S + 15]), reads=allut + ["U32h"], writes=["poolhalo"])
            tmpA = FB[:, D:D + 15 + SMAX]
            tmpB = FB[:, D + 1024:D + 1024 + 15 + SMAX]
            for c in range(16):
                g = c // 4
                nlev = g + 1
                src = U32[:, c, :]
                cur = src
                lo = 0
                tmps = [tmpA, tmpB]
                for lev in range(nlev):
                    sh = 1 << lev
                    dst = tmps[lev % 2]
                    lo2 = lo + sh
                    P.op("vector", lambda e, dst=dst, cur=cur, lo2=lo2, sh=sh: e.tensor_tensor(
                        out=dst[:, lo2:15 + S], in0=cur[:, lo2:15 + S], in1=cur[:, lo2 - sh:15 + S - sh], op=ALU.add),
                        reads=allut + ["U32h", "ptmp"], writes=["ptmp"])
                    cur = dst
                    lo = lo2
                if pidx == 0:
                    P.op("vector", lambda e, cur=cur, g=g: e.tensor_tensor(
                        out=cur[:, 15:31], in0=cur[:, 15:31], in1=pcn[:, g * 16:(g + 1) * 16], op=ALU.mult),
                        reads=["ptmp", "pcn"], writes=["ptmp"])
                if True:
                    P.op("vector", lambda e, cur=cur, c=c, src=src, g=g: e.scalar_tensor_tensor(
                        out=UT[:, c, 0:S], in0=cur[:, 15:15 + S], scalar=1.0 / (2 << g), in1=src[:, 15:15 + S],
                        op0=ALU.mult, op1=ALU.subtract),
                        reads=["ptmp"] + allut, writes=[("MIX", c)])
            pws = [wload(poolw_d[lj, 2 * q:2 * q + 2].rearrange("g (k p) n -> p (g k) n", p=128), (8, 512)) for q in range(2)]
            bw, bwtok = load_wbc(pb_d[lj:lj + 1, :])
            sw, swtok = load_wbc(psc_d[lj:lj + 1, :])
            P.barrier()
            col = 0
            for ti, (row0, r) in enumerate(part):
                for g in range(4):
                    b = nb()
                    pw, pwtok = pws[g // 2]
                    for kk in range(4):
                        P.op("tensor", lambda e, b=b, g=g, kk=kk, col=col, r=r, pw=pw: e.matmul(
                            banks[b][:r, :], lhsT=UT[:, g * 4 + kk, col:col + r], rhs=pw[:, (g % 2) * 4 + kk, :],
                            start=(kk == 0), stop=(kk == 3)), reads=[pwtok] + [("MIX", g * 4 + kk)], writes=[("pb", b)], banks=[b])
                    fb = FB[:r, ti * D + g * 512: ti * D + (g + 1) * 512]
                    P.op("vector", lambda e, fb=fb, b=b, g=g, r=r: e.tensor_tensor(
                        out=fb, in0=banks[b][:r, :], in1=bw[:r, g * 512:(g + 1) * 512], op=ALU.add),
                        reads=[("pb", b), bwtok], writes=[("FB", ti)], banks=[b])
                    P.op("vector", lambda e, fb=fb, g=g, r=r: e.tensor_tensor(
                        out=fb, in0=fb, in1=sw[:r, g * 512:(g + 1) * 512], op=ALU.mult),
                        reads=[("FB", ti), swtok], writes=[("FB", ti)])
                col += r
            postnorm(part, li, 1)

        def ssd(part, li, pidx, last_part):
            lj = li // 2
            S = sum(r for _, r in part)
            T = len(part)
            cols = []
            c_ = 0
            for (_, r) in part:
                cols.append(c_)
                c_ += r
            prenorm(part, li, 0)
            P.barrier()
            if stage <= 1:
                return
            allut = [("UT", ti) for ti in range(T)]
            off = [0]

            def carve(nelem, dt):
                nbytes = nelem * (4 if dt == F32 else 2)
                o = off[0]
                off[0] += (nbytes + 31) // 32 * 32
                assert off[0] <= TMAX * D * 4
                if dt == F32:
                    return FB[:, o // 4:o // 4 + nelem]
                return FB[:].bitcast(BF16)[:, o // 2:o // 2 + nelem]

            DT = carve(T * NH, F32)
            DA = carve(T * NH, F32)
            CS = carve(T * NH, F32)
            NCS = carve(T * NH, F32)
            ECS = carve(T * NH, F32)
            DST = carve(T * NH, F32)
            CDEC = carve(T * NH, F32)
            TMP64 = carve(NH, F32)
            TMP64b = carve(NH, F32)
            ZS = carve(T * 512, BF16)
            XPRE = [carve(3 + SMAX, F32) for _ in range(2)]
            CACC = [carve(SMAX, F32) for _ in range(1)]
            XT = carve(4 * SMAX, BF16)
            BT = carve(SMAX, BF16)
            CT = carve(SMAX, BF16)
            XTOK = [carve(512, BF16) for _ in range(2)]
            XDT = [carve(512, BF16) for _ in range(2)]
            XDS = [carve(512, BF16) for _ in range(1)]
            BTOK = [carve(128, BF16) for _ in range(2)]
            CBT = [carve(128, BF16) for _ in range(2)]
            EE = [carve(128, BF16) for _ in range(4)]
            MT = [carve(128, BF16) for _ in range(4)]
            bo = 32 * SMAX * 2
            BIGf = BIG[:].bitcast(F32)
            T1 = [BIGf[:, bo // 4 + i * 512: bo // 4 + (i + 1) * 512] for i in range(2)]
            T2 = [BIGf[:, bo // 4 + 1024: bo // 4 + 1536]]
            YN = [BIG[:, bo // 2 + 3072 + i * 512: bo // 2 + 3072 + (i + 1) * 512] for i in range(2)]
            assert bo + 6144 + 2048 <= 44 * SMAX * 2
            SST = carve(512, F32)
            SBF = carve(512, BF16)
            NWG = [carve(512, F32) for _ in range(1)]
            rot = lambda lst, nm: Rot([(lst[i], (nm, i)) for i in range(len(lst))])
            xpre_rot, cacc_rot = rot(XPRE, "XPRE"), rot(CACC, "CACC")
            xtok_rot, xdt_rot, xds_rot = rot(XTOK, "XTOK"), rot(XDT, "XDT"), rot(XDS, "XDS")
            btok_rot, cbt_rot, ee_rot, mt_rot = rot(BTOK, "BTOK"), rot(CBT, "CBT"), rot(EE, "EE"), rot(MT, "MT")
            t1_rot, t2_rot, yn_rot, nwg_rot = rot(T1, "T1"), rot(T2, "T2"), rot(YN, "YN"), rot(NWG, "NWG")
            YT = BIG[:, 0:32 * SMAX].rearrange("p (c t) -> p c t", c=32)
            aneg = ssdc[:, (lj * 3) * NH:(lj * 3 + 1) * NH]
            dskip = ssdc[:, (lj * 3 + 1) * NH:(lj * 3 + 2) * NH]
            dtbias = ssdc[:, (lj * 3 + 2) * NH:(lj * 3 + 3) * NH]
            winv = win_d[lj].rearrange("(k p) n -> p k n", p=128)
            wdt, wdttok = wload(winv[:, :, DI + 6144:DINP], (16, NH))
            for ti, (row0, r) in enumerate(part):
                c0 = cols[ti]
                sl = slice(ti * NH, (ti + 1) * NH)
                b = nb()
                for k in range(16):
                    P.op("tensor", lambda e, b=b, k=k, c0=c0, r=r: e.matmul(
                        banks[b][:r, 0:NH], lhsT=UT[:, k, c0:c0 + r], rhs=wdt[:, k, :], start=(k == 0), stop=(k == 15)),
                        reads=[wdttok, ("UT", ti)], writes=[("pb", b)], banks=[b])
                P.op("vector", lambda e, b=b, r=r: e.tensor_tensor(out=TMP64[:r, :], in0=banks[b][:r, 0:NH], in1=dtbias[:r, :], op=ALU.add),
                     reads=[("pb", b), ("ssdc", lj)], writes=["TMP64"], banks=[b])
                P.op("scalar", lambda e, r=r: e.activation(out=TMP64b[:r, :], in_=TMP64[:r, :], func=AF.Abs),
                     reads=["TMP64"], writes=["TMP64b"])
                P.op("scalar", lambda e, r=r: e.activation(out=TMP64b[:r, :], in_=TMP64b[:r, :], func=AF.Exp, scale=-1.0),
                     reads=["TMP64b"], writes=["TMP64b"])
                P.op("scalar", lambda e, r=r: e.activation(out=TMP64b[:r, :], in_=TMP64b[:r, :], func=AF.Ln, bias=1.0),
                     reads=["TMP64b"], writes=["TMP64b"])
                P.op("vector", lambda e, r=r, sl=sl: e.scalar_tensor_tensor(
                    out=DT[:r, sl], in0=TMP64[:r, :], scalar=0.0, in1=TMP64b[:r, :], op0=ALU.max, op1=ALU.add),
                    reads=["TMP64", "TMP64b"], writes=[("DT", ti)])
                P.op("vector", lambda e, r=r, sl=sl: e.tensor_tensor(out=DA[:r, sl], in0=DT[:r, sl], in1=aneg[:r, :], op=ALU.mult),
                     reads=[("DT", ti), ("ssdc", lj)], writes=[("DA", ti)])
                b = nb()
                P.op("tensor", lambda e, b=b, r=r, sl=sl: e.matmul(banks[b][:r, 0:NH], lhsT=tri32[:r, :r], rhs=DA[:r, sl], start=True, stop=True),
                     reads=[("DA", ti), "CST"], writes=[("pb", b)], banks=[b])
                P.op("tensor", lambda e, b=b, r=r, sl=sl: e.matmul(banks[b][:, NH:2 * NH], lhsT=ones32[:r, :], rhs=DA[:r, sl], start=True, stop=True),
                     reads=[("DA", ti), "CST"], writes=[("pb", b)], banks=[b])
                P.op("vector", lambda e, b=b, r=r, sl=sl: e.tensor_copy(out=CS[:r, sl], in_=banks[b][:r, 0:NH]),
                     reads=[("pb", b)], writes=[("CS", ti)], banks=[b])
                P.op("vector", lambda e, b=b, r=r, sl=sl: e.tensor_scalar(out=NCS[:r, sl], in0=banks[b][:r, 0:NH], scalar1=-1.0, scalar2=None, op0=ALU.mult),
                     reads=[("pb", b)], writes=[("NCS", ti)], banks=[b])
                P.op("scalar", lambda e, b=b, r=r, sl=sl: e.activation(out=ECS[:r, sl], in_=banks[b][:r, 0:NH], func=AF.Exp),
                     reads=[("pb", b)], writes=[("ECS", ti)], banks=[b])
                P.op("scalar", lambda e, b=b, sl=sl: e.activation(out=CDEC[:, sl], in_=banks[b][:, NH:2 * NH], func=AF.Exp),
                     reads=[("pb", b)], writes=[("CDEC", ti)], banks=[b])
                P.op("vector", lambda e, b=b, r=r, sl=sl: e.tensor_tensor(out=DST[:r, sl], in0=banks[b][:r, NH:2 * NH], in1=CS[:r, sl], op=ALU.subtract),
                     reads=[("pb", b), ("CS", ti)], writes=[("DST", ti)], banks=[b])
                P.op("scalar", lambda e, r=r, sl=sl: e.activation(out=DST[:r, sl], in_=DST[:r, sl], func=AF.Exp),
                     reads=[("DST", ti)], writes=[("DST", ti)])
            if stage <= 2:
                return
            for g in range(NG if stage > 3 else 1):
                strow = (lj * NG + g) * 128
                if pidx == 0:
                    P.op("vector", lambda e: e.memset(SST, 0.0), writes=["SST"])
                else:
                    P.dma("sync", SST, st_d[strow:strow + 128, :], reads=[("std", lj, g)], writes=["SST"])
                P.op("scalar", lambda e: e.copy(out=SBF, in_=SST), reads=["SST"], writes=["SBF"])
                nwg, nwgtok = nwg_rot.next()
                P.dma("sync", nwg, snw_d[lj:lj + 1, g * 512:(g + 1) * 512].partition_broadcast(128), writes=[nwgtok])
                for hb in range(2):
                    wz, wztok = wload(winv[:, :, g * 512 + hb * 256: g * 512 + (hb + 1) * 256], (16, 256))
                    for ti, (row0, r) in enumerate(part):
                        b = nb()
                        for k in range(16):
                            P.op("tensor", lambda e, b=b, k=k, wz=wz, c0=cols[ti], r=r: e.matmul(
                                banks[b][:r, 0:256], lhsT=UT[:, k, c0:c0 + r], rhs=wz[:, k, :], start=(k == 0), stop=(k == 15)),
                                reads=[wztok, ("UT", ti)], writes=[("pb", b)], banks=[b])
                        P.op("scalar", lambda e, b=b, ti=ti, r=r, hb=hb: e.activation(
                            out=ZS[:r, ti * 512 + hb * 256: ti * 512 + (hb + 1) * 256], in_=banks[b][:r, 0:256], func=AF.Silu),
                            reads=[("pb", b)], writes=[("ZS", ti)], banks=[b])
                chunk_list = [(DI + g * 512 + c * 128, 4 * g + c, XT[:, c * SMAX:(c + 1) * SMAX], ("XT", c)) for c in range(4)]
                chunk_list.append((DI + DI + g * 128, 32 + g, BT, "BT"))
                chunk_list.append((DI + DI + 1024 + g * 128, 40 + g, CT, "CT"))
                wblocks = {}
                for (colw, cch, dst, dtok) in chunk_list:
                    blk = colw // 256
                    if blk not in wblocks:
                        wblocks[blk] = wload(winv[:, :, blk * 256:(blk + 1) * 256], (16, 256))
                    wx, wxtok = wblocks[blk]
                    o_in = colw - blk * 256
                    xp, xptok = xpre_rot.next()
                    hal = convhalo[:, (lj * 48 + cch) * 3:(lj * 48 + cch) * 3 + 3]
                    P.op("vector", lambda e, xp=xp, hal=hal: e.tensor_copy(out=xp[:, 0:3], in_=hal), reads=[("chalo", lj, cch)], writes=[xptok])
                    for (n0, n1) in tokgroups(S):
                        b = nb()
                        for k in range(16):
                            P.op("tensor", lambda e, b=b, k=k, wx=wx, o_in=o_in, n0=n0, n1=n1: e.matmul(
                                banks[b][:, 0:n1 - n0], lhsT=wx[:, k, o_in:o_in + 128], rhs=UT[:, k, n0:n1], start=(k == 0), stop=(k == 15)),
                                reads=[wxtok] + allut, writes=[("pb", b)], banks=[b])
                        copy_op(evac_eng(), xp[:, 3 + n0:3 + n1], banks[b][:, 0:n1 - n0], reads=[("pb", b)], writes=[xptok], banks=[b])
                    P.op("vector", lambda e, xp=xp, hal=hal: e.tensor_copy(out=hal, in_=xp[:, S:S + 3]), reads=[xptok], writes=[("chalo", lj, cch)])
                    ca, catok = cacc_rot.next()
                    cwb = (lj * 48 + cch) * 4
                    P.op("vector", lambda e, ca=ca, xp=xp, cwb=cwb, cch=cch: e.tensor_scalar(
                        out=ca[:, 0:S], in0=xp[:, 0:S], scalar1=cwt[:, cwb:cwb + 1], scalar2=cbt[:, lj * 48 + cch:lj * 48 + cch + 1],
                        op0=ALU.mult, op1=ALU.add), reads=[xptok, "cwt", "cbt"], writes=[catok])
                    for kq in range(1, 4):
                        P.op("vector", lambda e, ca=ca, xp=xp, cwb=cwb, kq=kq: e.scalar_tensor_tensor(
                            out=ca[:, 0:S], in0=xp[:, kq:kq + S], scalar=cwt[:, cwb + kq:cwb + kq + 1], in1=ca[:, 0:S],
                            op0=ALU.mult, op1=ALU.add), reads=[xptok, catok, "cwt"], writes=[catok])
                    P.op("scalar", lambda e, ca=ca, dst=dst: e.activation(out=dst[:, 0:S], in_=ca[:, 0:S], func=AF.Silu),
                         reads=[catok], writes=[dtok])
                for ti, (row0, r) in enumerate(part):
                    c0 = cols[ti]
                    sl = slice(ti * NH, (ti + 1) * NH)
                    hs = slice(ti * NH + g * 8, ti * NH + g * 8 + 8)
                    b = nb()
                    pbv = banks[b][:].bitcast(BF16)
                    for c in range(4):
                        P.op("tensor", lambda e, pbv=pbv, c=c, c0=c0, r=r: e.transpose(
                            out=pbv[:r, c * 128:(c + 1) * 128], in_=XT[:, c * SMAX + c0:c * SMAX + c0 + r], identity=identb[:, :]),
                            reads=[("XT", c), "identb"], writes=[("pb", b)], banks=[b], cost=90.0)
                    P.op("tensor", lambda e, pbv=pbv, c0=c0, r=r: e.transpose(
                        out=pbv[:r, 512:640], in_=BT[:, c0:c0 + r], identity=identb[:, :]),
                        reads=["BT", "identb"], writes=[("pb", b)], banks=[b], cost=90.0)
                    xtk, xtktok = xtok_rot.next()
                    xdt, xdttok = xdt_rot.next()
                    xds, xdstok = xds_rot.next()
                    btk, btktok = btok_rot.next()
                    P.op("scalar", lambda e, xtk=xtk, pbv=pbv, r=r: e.copy(out=xtk[:r, :], in_=pbv[:r, 0:512]),
                         reads=[("pb", b)], writes=[xtktok], banks=[b])
                    P.op("scalar", lambda e, btk=btk, pbv=pbv, r=r: e.copy(out=btk[:r, :], in_=pbv[:r, 512:640]),
                         reads=[("pb", b)], writes=[btktok], banks=[b])
                    P.op("vector", lambda e, xdt=xdt, xtk=xtk, r=r, hs=hs: e.tensor_tensor(
                        out=xdt[:r, :].rearrange("p (h d) -> p h d", h=8), in0=xtk[:r, :].rearrange("p (h d) -> p h d", h=8),
                        in1=DT[:r, hs].unsqueeze(2).to_broadcast([r, 8, HD]), op=ALU.mult),
                        reads=[xtktok, ("DT", ti)], writes=[xdttok])
                    P.op("vector", lambda e, xds=xds, xdt=xdt, r=r, hs=hs: e.tensor_tensor(
                        out=xds[:r, :].rearrange("p (h d) -> p h d", h=8), in0=xdt[:r, :].rearrange("p (h d) -> p h d", h=8),
                        in1=DST[:r, hs].unsqueeze(2).to_broadcast([r, 8, HD]), op=ALU.mult),
                        reads=[xdttok, ("DST", ti)], writes=[xdstok])
                    b = nb()
                    P.op("tensor", lambda e, b=b, c0=c0, r=r: e.matmul(banks[b][:r, 0:r], lhsT=BT[:, c0:c0 + r], rhs=CT[:, c0:c0 + r], start=True, stop=True),
                         reads=["BT", "CT"], writes=[("pb", b)], banks=[b])
                    cbt_, cbttok = cbt_rot.next()
                    P.op("vector", lambda e, cbt_=cbt_, b=b, r=r: e.tensor_copy(out=cbt_[:r, 0:r], in_=banks[b][:r, 0:r]),
                         reads=[("pb", b)], writes=[cbttok], banks=[b])
                    byo = nb()
                    P.op("tensor", lambda e, byo=byo, c0=c0, r=r: e.matmul(banks[byo][:r, :], lhsT=CT[:, c0:c0 + r], rhs=SBF, start=True, stop=True),
                         reads=["CT", "SBF"], writes=[("pb", byo)], banks=[byo], cost=215.0)
                    t1, t1tok = t1_rot.next()
                    P.op("vector", lambda e, t1=t1, byo=byo, r=r, hs=hs: e.tensor_tensor(
                        out=t1[:r, :].rearrange("p (h d) -> p h d", h=8), in0=banks[byo][:r, :].rearrange("p (h d) -> p h d", h=8),
                        in1=ECS[:r, hs].unsqueeze(2).to_broadcast([r, 8, HD]), op=ALU.mult),
                        reads=[("pb", byo), ("ECS", ti)], writes=[t1tok], banks=[byo])
                    by = nb()
                    mts = []
                    for half in range(2):
                        bs = nb()
                        for hh in range(4):
                            h = half * 4 + hh
                            hcol = ti * NH + g * 8 + h
                            P.op("tensor", lambda e, bs=bs, hh=hh, hcol=hcol, r=r: e.matmul(
                                banks[bs][:r, hh * 128:hh * 128 + r], lhsT=DA[:r, hcol:hcol + 1].to_broadcast([r, r]), rhs=tri32[:r, :r],
                                start=True, stop=False), reads=[("DA", ti), "CST"], writes=[("pb", bs)], banks=[bs], cost=225.0)
                            P.op("tensor", lambda e, bs=bs, hh=hh, r=r: e.matmul(
                                banks[bs][:r, hh * 128:hh * 128 + r], lhsT=identb[:r, :r], rhs=maskb[:r, :r],
                                start=False, stop=True), reads=["identb", "maskb"], writes=[("pb", bs)], banks=[bs], cost=60.0)
                        for hh in range(4):
                            h = half * 4 + hh
                            hcol = ti * NH + g * 8 + h
                            ee, eetok = ee_rot.next()
                            mt, mttok = mt_rot.next()
                            P.op("scalar", lambda e, ee=ee, bs=bs, hh=hh, hcol=hcol, r=r: e.activation(
                                out=ee[:r, 0:r], in_=banks[bs][:r, hh * 128:hh * 128 + r], func=AF.Exp, bias=NCS[:r, hcol:hcol + 1]),
                                reads=[("pb", bs), ("NCS", ti)], writes=[eetok], banks=[bs], cost=330.0)
                            P.op("vector", lambda e, mt=mt, ee=ee, cbt_=cbt_, r=r: e.tensor_tensor(
                                out=mt[:r, 0:r], in0=ee[:r, 0:r], in1=cbt_[:r, 0:r], op=ALU.mult),
                                reads=[eetok, cbttok], writes=[mttok], cost=220.0)
                            P.op("tensor", lambda e, by=by, mt=mt, xdt=xdt, h=h, r=r: e.matmul(
                                banks[by][:r, h * HD:(h + 1) * HD], lhsT=mt[:r, 0:r], rhs=xdt[:r, h * HD:(h + 1) * HD], start=True, stop=True),
                                reads=[mttok, xdttok], writes=[("pb", by)], banks=[by], cost=45.0)
                    P.op("vector", lambda e, t1=t1, by=by, r=r: e.tensor_tensor(out=t1[:r, :], in0=t1[:r, :], in1=banks[by][:r, :], op=ALU.add),
                         reads=[t1tok, ("pb", by)], writes=[t1tok], banks=[by])
                    t2, t2tok = t2_rot.next()
                    P.op("vector", lambda e, t2=t2, xtk=xtk, r=r, g=g: e.tensor_tensor(
                        out=t2[:r, :].rearrange("p (h d) -> p h d", h=8), in0=xtk[:r, :].rearrange("p (h d) -> p h d", h=8),
                        in1=dskip[:r, g * 8:g * 8 + 8].unsqueeze(2).to_broadcast([r, 8, HD]), op=ALU.mult),
                        reads=[xtktok, ("ssdc", lj)], writes=[t2tok])
                    P.op("vector", lambda e, t1=t1, t2=t2, r=r: e.tensor_tensor(out=t1[:r, :], in0=t1[:r, :], in1=t2[:r, :], op=ALU.add),
                         reads=[t1tok, t2tok], writes=[t1tok])
                    P.op("vector", lambda e, t1=t1, ti=ti, r=r: e.tensor_tensor(out=t1[:r, :], in0=t1[:r, :], in1=ZS[:r, ti * 512:(ti + 1) * 512], op=ALU.mult),
                         reads=[t1tok, ("ZS", ti)], writes=[t1tok])
                    scol = 5 + (ti % 2)
                    yn, yntok = yn_rot.next()
                    sumsq(t1[:r, :], r, scol, [t1tok], yn[:r, :], [yntok])
                    rstd(r, scol, 512)
                    P.op("vector", lambda e, yn=yn, t1=t1, r=r, scol=scol, nwg=nwg: e.scalar_tensor_tensor(
                        out=yn[:r, :], in0=t1[:r, :], scalar=rsd[:r, scol:scol + 1], in1=nwg[:r, :], op0=ALU.mult, op1=ALU.mult),
                        reads=[t1tok, ("rsd", scol), nwgtok], writes=[yntok])
                    b = nb()
                    pbv2 = banks[b][:].bitcast(BF16)
                    for c in range(4):
                        P.op("tensor", lambda e, pbv2=pbv2, c=c, yn=yn, r=r: e.transpose(
                            out=pbv2[:, c * 128:c * 128 + r], in_=yn[:r, c * 128:(c + 1) * 128], identity=identb[:r, :r]),
                            reads=[yntok, "identb"], writes=[("pb", b)], banks=[b], cost=90.0)
                    copy_op(evac_eng(), YT[:, g * 4:g * 4 + 4, c0:c0 + r], pbv2[:, 0:512].rearrange("p (k t) -> p k t", k=4)[:, :, 0:r],
                            reads=[("pb", b)], writes=[("YT", g, ti)], banks=[b])
                    b = nb()
                    P.op("tensor", lambda e, b=b, btk=btk, xds=xds, r=r: e.matmul(banks[b][:, :], lhsT=btk[:r, :], rhs=xds[:r, :], start=True, stop=True),
                         reads=[btktok, xdstok], writes=[("pb", b)], banks=[b], cost=215.0)
                    P.op("vector", lambda e, hs=hs: e.tensor_tensor(
                        out=SST.rearrange("p (h d) -> p h d", h=8), in0=SST.rearrange("p (h d) -> p h d", h=8),
                        in1=CDEC[:, hs].unsqueeze(2).to_broadcast([128, 8, HD]), op=ALU.mult),
                        reads=["SST", ("CDEC", ti)], writes=["SST"])
                    P.op("vector", lambda e, b=b: e.tensor_tensor(out=SST, in0=SST, in1=banks[b][:, :], op=ALU.add),
                         reads=["SST", ("pb", b)], writes=["SST"], banks=[b])
                    P.op("scalar", lambda e: e.copy(out=SBF, in_=SST), reads=["SST"], writes=["SBF"])
                if not last_part:
                    P.dma("sync", st_d[strow:strow + 128, :], SST, reads=["SST"], writes=[("std", lj, g)])
            P.barrier()
            if stage <= 4:
                return
            woutv = wout_d[lj].rearrange("(k p) n -> p k n", p=128)
            for nblk in range(4):
                accs = [nb() for _ in part]
                for pc in range(4):
                    wo, wotok = wload(woutv[:, pc * 8:pc * 8 + 8, nblk * 512:(nblk + 1) * 512], (8, 512))
                    for kk in range(8):
                        ch = pc * 8 + kk
                        for ti, (row0, r) in enumerate(part):
                            P.op("tensor", lambda e, a=accs[ti], ch=ch, kk=kk, wo=wo, c0=cols[ti], r=r: e.matmul(
                                banks[a][:r, :], lhsT=YT[:, ch, c0:c0 + r], rhs=wo[:, kk, :], start=(ch == 0), stop=(ch == 31)),
                                reads=[wotok, ("YT", ch // 4, ti)], writes=[("pb", accs[ti])], banks=[accs[ti]], cost=215.0)
                for ti, (row0, r) in enumerate(part):
                    copy_op(evac_eng(), FB[:r, ti * D + nblk * 512: ti * D + (nblk + 1) * 512], banks[accs[ti]][:r, :],
                            reads=[("pb", accs[ti])], writes=[("FB", ti)], banks=[accs[ti]])
            postnorm(part, li, 1)

        for pidx, part in enumerate(parts):
            last_part = pidx == len(parts) - 1
            for ti, (row0, r) in enumerate(part):
                src = meta_d[0:NMETA, :] if row0 == 0 else x_d[row0 - NMETA:row0 - NMETA + r, :]
                P.dma("sync", H[:r, ti, :], src, writes=[("H", ti)])
            for li in range(layers if stage > 0 else 0):
                if li % 2 == 0:
                    ssd(part, li, pidx, last_part)
                else:
                    pool(part, li, pidx)
                P.barrier()
                if stage > 5:
                    ffn(part, li)
                P.barrier()
            for ti, (row0, r) in enumerate(part):
                if row0 == 0:
                    continue
                P.dma("sync", y_d[row0 - NMETA:row0 - NMETA + r, :], H[:r, ti, :], reads=[("H", ti)], writes=[("y", pidx, ti)])
        P.op("sync", None, reads=[("y", pidx, ti) for pidx, part in enumerate(parts) for ti, (row0, r) in enumerate(part) if row0 != 0])
        if (dbg or {}).get("sched", True):
            P.schedule()
        P.emit(st)
    return nc


def host_consts():
    cst = np.zeros((128, 512), np.float32)
    cst[:, 0:128] = np.eye(128, dtype=np.float32)
    cst[:, 128:256] = np.triu(np.ones((128, 128), np.float32))
    cst[:, 256:384] = 1.0
    cst[:, 384:512] = np.tril(np.full((128, 128), -30000.0, np.float32), -1)
    pcnt = np.zeros((1, 64), np.float32)
    for g, win in enumerate((2, 4, 8, 16)):
        t = np.arange(16)
        pcnt[0, g * 16:(g + 1) * 16] = win / np.minimum(t + 1, win)
    return cst, pcnt


_NC_CACHE = {}


def kernel(x, meta_tokens, norm_w, ssd_w_in, ssd_conv_w, ssd_conv_b, ssd_dt_bias, ssd_a_log, ssd_d,
           ssd_norm_w, ssd_w_out, pool_w, pool_b, pool_scale, ffn_w_gate, ffn_w_up, ffn_w_down):
    f = lambda a: np.ascontiguousarray(np.asarray(a, dtype=np.float32))
    x = f(x)
    B, nx, _ = x.shape
    key = (nx, DEPTH)
    if key not in _NC_CACHE:
        _NC_CACHE[key] = build(nx, DEPTH)
    nc = _NC_CACHE[key]
    cst, pcnt = host_consts()
    cw = f(ssd_conv_w)
    cw = np.ascontiguousarray(cw.reshape(2, 4, 48, 128).transpose(0, 3, 2, 1).reshape(2, 128, 48 * 4))
    cb = f(ssd_conv_b)
    cb = np.ascontiguousarray(cb.reshape(2, 48, 128).transpose(0, 2, 1))
    shared = {
        "meta": f(meta_tokens), "norm_w": f(norm_w), "ssd_w_in": f(ssd_w_in), "ssd_w_out": f(ssd_w_out),
        "pool_w": f(pool_w), "ffn_w_gate": f(ffn_w_gate), "ffn_w_up": f(ffn_w_up), "ffn_w_down": f(ffn_w_down),
        "ssd_cw": cw, "ssd_cb": cb, "ssd_dt_bias": f(ssd_dt_bias), "ssd_a_log": f(ssd_a_log), "ssd_d": f(ssd_d),
        "ssd_norm_w": f(ssd_norm_w), "pool_b": f(pool_b), "pool_scale": f(pool_scale), "cst": cst, "pcnt": pcnt,
    }
    ncores = 8
    in_maps = []
    for c in range(ncores):
        m = dict(shared)
        m["x"] = x[c % B]
        in_maps.append(m)
    res = run_bass_kernel_spmd(nc, in_maps, core_ids=list(range(ncores)))
    out = np.stack([res.results[b]["y"] for b in range(B)], axis=0)
    return out.astype(np.float32)
```
